# Optimizing a Trainium2 kernel written in Bass

```python
import jax, jax.numpy as jnp
from jax import lax
import numpy as np

D_MODEL = 1024
BATCH = 8
SEQ = 2048
DEPTH = 4
DEC_BATCH = 128
DEC_SEQ = 1
PAST_LEN = 16384
PAGE_SIZE = 128

N_EVEN = (DEPTH + 1) // 2
N_ODD = DEPTH // 2
CHUNK = 128
W_A = D_MODEL // 2
H_A = 8
DH_A = W_A // H_A
W_B = D_MODEL // 2
H_B = 8
DH_B = W_B // H_B
CONV_W = 4
LRU_C = 8.0
D_IN = 2 * W_A + 2 * W_B
POOL_WINDOWS = (2, 4, 8, 16)
N_POOL = len(POOL_WINDOWS)
G_POOL = D_MODEL // N_POOL
POOL_HIST = max(POOL_WINDOWS) - 1
D_FF = 4 * D_MODEL
EPS = 1e-6

kernel_name = 'hybrid_sgu_rglru_pool_decoder_step'


def rmsnorm(x, g):
    xf = x.astype(jnp.float32)
    y = xf * lax.rsqrt(jnp.mean(xf * xf, axis=-1, keepdims=True) + EPS)
    return (y * g.astype(jnp.float32)).astype(x.dtype)


def chunk_spatial_gate(u, v, ws, bs):
    b, l, _ = v.shape
    n_chunks = -(-l // CHUNK)
    pad = n_chunks * CHUNK - l
    vp = jnp.pad(v, ((0, 0), (0, pad), (0, 0))).reshape(b, n_chunks, CHUNK, H_A, DH_A)
    mask = jnp.tril(jnp.ones((CHUNK, CHUNK), dtype=bool))
    wm = jnp.where(mask, ws, 0).astype(v.dtype)
    s = jnp.einsum('hts,bcshd->bcthd', wm, vp) + bs.T.astype(v.dtype)[None, None, :, :, None]
    s = s.reshape(b, n_chunks * CHUNK, W_A)[:, :l]
    return u * s


def causal_conv(x, hist, w, bias):
    xx = jnp.concatenate([hist.astype(x.dtype), x], axis=1)
    l = x.shape[1]
    y = bias + sum(xx[:, k:k + l] * w[k] for k in range(CONV_W))
    return y, xx[:, -(CONV_W - 1):]


def rg_lru(xc, h0, wa, ba, wx, bx, lam):
    b, l, _ = xc.shape
    xh = xc.reshape(b, l, H_B, DH_B)
    r = jax.nn.sigmoid(jnp.einsum('blhi,hij->blhj', xh, wa).reshape(b, l, W_B) + ba)
    i = jax.nn.sigmoid(jnp.einsum('blhi,hij->blhj', xh, wx).reshape(b, l, W_B) + bx)
    log_a = (-LRU_C * r.astype(jnp.float32)) * jax.nn.softplus(-lam.astype(jnp.float32))
    a = jnp.exp(log_a)
    mult = jnp.sqrt(-jnp.expm1(2.0 * log_a))
    bt = mult * (i * xc).astype(jnp.float32)

    def step(h, ab):
        a_t, b_t = ab
        h = a_t * h + b_t
        return h, h

    h_last, hs = lax.scan(step, h0.astype(jnp.float32), (jnp.swapaxes(a, 0, 1), jnp.swapaxes(bt, 0, 1)))
    return jnp.swapaxes(hs, 0, 1).astype(xc.dtype), h_last.astype(h0.dtype)


def even_mixer(xn, conv_hist, h0, w_in, w_out, v_norm, sgu_w, sgu_b, conv_w, conv_b,
               gate_a_w, gate_a_b, gate_x_w, gate_x_b, lru_lambda):
    proj = xn @ w_in
    u = jax.nn.gelu(proj[..., :W_A])
    v = rmsnorm(jax.nn.gelu(proj[..., W_A:2 * W_A]), v_norm)
    gate = proj[..., 2 * W_A:2 * W_A + W_B]
    xb = proj[..., 2 * W_A + W_B:]
    a_out = chunk_spatial_gate(u, v, sgu_w, sgu_b)
    xc, conv_new = causal_conv(xb, conv_hist, conv_w, conv_b)
    hs, h_new = rg_lru(xc, h0, gate_a_w, gate_a_b, gate_x_w, gate_x_b, lru_lambda)
    b_out = hs * jax.nn.gelu(gate)
    y = jnp.concatenate([a_out, b_out], axis=-1) @ w_out
    return y, v, conv_new, h_new


def multi_pool(xn, hist, start_pos, wp, bp, scale):
    b, l, d = xn.shape
    z = jnp.concatenate([hist.astype(xn.dtype), xn], axis=1)
    zf = z.astype(jnp.float32)
    cs = jnp.concatenate([jnp.zeros((b, 1, d), jnp.float32), jnp.cumsum(zf, axis=1)], axis=1)
    pos = start_pos + jnp.arange(l)
    outs = []
    for g, w in enumerate(POOL_WINDOWS):
        sl = slice(g * G_POOL, (g + 1) * G_POOL)
        wsum = cs[:, POOL_HIST + 1:POOL_HIST + 1 + l, sl] - cs[:, POOL_HIST + 1 - w:POOL_HIST + 1 - w + l, sl]
        cnt = jnp.minimum(pos + 1, w).astype(jnp.float32)[None, :, None]
        outs.append(wsum / cnt - xn[..., sl].astype(jnp.float32))
    p = jnp.stack(outs, axis=2).astype(xn.dtype)
    y = jnp.einsum('blgi,gij->blgj', p, wp).reshape(b, l, d) + bp
    return y * scale, z[:, -POOL_HIST:]


def channel_mlp(xn, w1, w2):
    return jnp.square(jax.nn.relu(xn @ w1)) @ w2


def setup_inputs(seed: int = 0) -> dict:
    key = jax.random.key(seed)
    ks = jax.random.split(key, 32)
    f32 = jnp.float32

    def nrm(k, shape, s):
        return jax.random.normal(k, shape, f32) * s

    a0 = jax.random.uniform(ks[15], (N_EVEN, W_B), f32, 0.9, 0.999)
    p = a0 ** (1.0 / LRU_C)
    lru_lambda = jnp.log(p) - jnp.log1p(-p)
    return {
        'x_prompt': nrm(ks[0], (BATCH, SEQ, D_MODEL), 1.0),
        'x_sample': nrm(ks[1], (DEC_BATCH, DEC_SEQ, D_MODEL), 1.0),
        'state_conv': nrm(ks[2], (N_EVEN, DEC_BATCH, CONV_W - 1, W_B), 1.0),
        'state_rglru': nrm(ks[3], (N_EVEN, DEC_BATCH, W_B), 0.5),
        'state_pool': nrm(ks[4], (N_ODD, DEC_BATCH, POOL_HIST, D_MODEL), 1.0),
        'norm_mix': 1.0 + nrm(ks[5], (DEPTH, D_MODEL), 0.05),
        'norm_ffn': 1.0 + nrm(ks[6], (DEPTH, D_MODEL), 0.05),
        'norm_final': 1.0 + nrm(ks[7], (D_MODEL,), 0.05),
        'w_in': nrm(ks[8], (N_EVEN, D_MODEL, D_IN), D_MODEL ** -0.5),
        'w_out': nrm(ks[9], (N_EVEN, W_A + W_B, D_MODEL), (W_A + W_B) ** -0.5),
        'v_norm': 1.0 + nrm(ks[10], (N_EVEN, W_A), 0.05),
        'sgu_w': nrm(ks[11], (N_EVEN, H_A, CHUNK, CHUNK), CHUNK ** -0.5),
        'sgu_b': 1.0 + nrm(ks[12], (N_EVEN, H_A, CHUNK), 0.1),
        'conv_w': nrm(ks[13], (N_EVEN, CONV_W, W_B), CONV_W ** -0.5),
        'conv_b': nrm(ks[14], (N_EVEN, W_B), 0.01),
        'gate_a_w': nrm(ks[16], (N_EVEN, H_B, DH_B, DH_B), DH_B ** -0.5),
        'gate_a_b': nrm(ks[17], (N_EVEN, W_B), 0.01),
        'gate_x_w': nrm(ks[18], (N_EVEN, H_B, DH_B, DH_B), DH_B ** -0.5),
        'gate_x_b': nrm(ks[19], (N_EVEN, W_B), 0.01),
        'lru_lambda': lru_lambda,
        'pool_w': nrm(ks[20], (N_ODD, N_POOL, G_POOL, G_POOL), G_POOL ** -0.5),
        'pool_b': nrm(ks[21], (N_ODD, D_MODEL), 0.01),
        'pool_scale': 1.0 + nrm(ks[22], (N_ODD, D_MODEL), 0.1),
        'ffn_w1': nrm(ks[23], (DEPTH, D_MODEL, D_FF), D_MODEL ** -0.5),
        'ffn_w2': nrm(ks[24], (DEPTH, D_FF, D_MODEL), D_FF ** -0.5),
    }


def reference(x_prompt, x_sample, state_conv, state_rglru, state_pool,
              norm_mix, norm_ffn, norm_final, w_in, w_out, v_norm, sgu_w, sgu_b,
              conv_w, conv_b, gate_a_w, gate_a_b, gate_x_w, gate_x_b, lru_lambda,
              pool_w, pool_b, pool_scale, ffn_w1, ffn_w2):
    xp, xs = x_prompt, x_sample
    n_p = xp.shape[0]
    sgu_v_s, conv_p, conv_s, h_p, h_s, pool_p, pool_s = [], [], [], [], [], [], []
    for layer in range(DEPTH):
        g = norm_mix[layer]
        if layer % 2 == 0:
            e = layer // 2
            prm = (w_in[e], w_out[e], v_norm[e], sgu_w[e], sgu_b[e], conv_w[e], conv_b[e],
                   gate_a_w[e], gate_a_b[e], gate_x_w[e], gate_x_b[e], lru_lambda[e])
            zero_conv = jnp.zeros((n_p, CONV_W - 1, W_B), xp.dtype)
            zero_h = jnp.zeros((n_p, W_B), state_rglru.dtype)
            yp, _, cp, hp = even_mixer(rmsnorm(xp, g), zero_conv, zero_h, *prm)
            ys, vs, cs_, hs_ = even_mixer(rmsnorm(xs, g), state_conv[e], state_rglru[e], *prm)
            sgu_v_s.append(vs)
            conv_p.append(cp)
            conv_s.append(cs_)
            h_p.append(hp)
            h_s.append(hs_)
        else:
            o = layer // 2
            zero_hist = jnp.zeros((n_p, POOL_HIST, D_MODEL), xp.dtype)
            yp, pp = multi_pool(rmsnorm(xp, g), zero_hist, 0, pool_w[o], pool_b[o], pool_scale[o])
            ys, ps = multi_pool(rmsnorm(xs, g), state_pool[o], PAST_LEN, pool_w[o], pool_b[o], pool_scale[o])
            pool_p.append(pp)
            pool_s.append(ps)
        xp = xp + yp
        xs = xs + ys
        xp = xp + channel_mlp(rmsnorm(xp, norm_ffn[layer]), ffn_w1[layer], ffn_w2[layer])
        xs = xs + channel_mlp(rmsnorm(xs, norm_ffn[layer]), ffn_w1[layer], ffn_w2[layer])
    y_prompt = rmsnorm(xp, norm_final)
    y_sample = rmsnorm(xs, norm_final)
    return (y_prompt, y_sample, jnp.stack(sgu_v_s), jnp.stack(conv_p), jnp.stack(conv_s),
            jnp.stack(h_p), jnp.stack(h_s), jnp.stack(pool_p), jnp.stack(pool_s))
```

```python
import contextlib
import numpy as np
import concourse.bass as bass
import concourse.mybir as mybir
from concourse.bass_utils import run_bass_kernel_spmd

F32 = mybir.dt.float32
BF16 = mybir.dt.bfloat16
AF = mybir.ActivationFunctionType
ALU = mybir.AluOpType

NCORES = 8
D = 1024
SEQ = 2048
NS = 16
NT = SEQ + NS
TILES = [(0, 512), (512, 512), (1024, 512), (1536, 512), (2048, 16)]
EPS = 1e-6
COMPUTE = ("pe", "act", "dve", "pool")
EPOCH = 24000
REGIONS = ("A", "B")


def C(*args, **kwargs):
    return (args, kwargs)


class Op:
    __slots__ = ("eng", "fn", "r", "w", "dma", "deps", "signal", "tok", "idx", "name")

    def __init__(self, eng, fn, r, w, dma, name):
        self.eng, self.fn, self.r, self.w, self.dma, self.name = eng, fn, tuple(r), tuple(w), dma, name
        self.deps = set()
        self.signal = False
        self.tok = None


class Sched:
    def __init__(self, nc, n_dma_sems=8):
        self.nc = nc
        self.ops = []
        self.last_w = {}
        self.readers = {}
        self.n_dma_sems = n_dma_sems
        self.region_ops = {r: [] for r in REGIONS}
        self.fence_of = {r: None for r in REGIONS}

    def op(self, eng, meth, call, r=(), w=(), dma=False, name=""):
        o = Op(eng, (meth, call), r, w, dma, name or meth)
        o.idx = len(self.ops)
        for k in o.r:
            lw = self.last_w.get(k)
            if lw is not None:
                o.deps.add(lw)
        for k in o.w:
            lw = self.last_w.get(k)
            if lw is not None:
                o.deps.add(lw)
            for rd in self.readers.get(k, ()):
                o.deps.add(rd)
        for k in o.r:
            self.readers.setdefault(k, []).append(o)
        for k in o.w:
            self.last_w[k] = o
            self.readers[k] = []
        regs = set()
        for k in o.r + o.w:
            if isinstance(k, tuple) and k[0] in self.region_ops:
                regs.add(k[0])
        for rg in regs:
            self.region_ops[rg].append(o)
            if self.fence_of[rg] is not None:
                o.deps.add(self.fence_of[rg])
        o.deps.discard(o)
        self.ops.append(o)
        return o

    def fence(self, region, meth, call):
        o = Op("pool", (meth, call), (), (), False, "fence_" + region)
        o.idx = len(self.ops)
        for p in self.region_ops[region]:
            o.deps.add(p)
        if self.fence_of[region] is not None:
            o.deps.add(self.fence_of[region])
        self.region_ops[region] = []
        self.fence_of[region] = o
        self.ops.append(o)
        return o

    def pe(self, meth, call, r=(), w=(), name=""):
        return self.op("pe", meth, call, r, w, name=name)

    def act(self, meth, call, r=(), w=(), name=""):
        return self.op("act", meth, call, r, w, name=name)

    def dve(self, meth, call, r=(), w=(), name=""):
        return self.op("dve", meth, call, r, w, name=name)

    def pool(self, meth, call, r=(), w=(), name=""):
        return self.op("pool", meth, call, r, w, name=name)

    def dma(self, meth, call, r=(), w=(), q="sp", name=""):
        return self.op(q, meth, call, r, w, dma=True, name=name)

    def _needs_sync(self, d, o):
        if d.dma:
            return True
        if d.eng != o.eng:
            return True
        if o.dma:
            return True
        if d.eng == "pe":
            return False
        if d.name.startswith("fence_"):
            return True
        return bool(set(d.w) & (set(o.r) | set(o.w))) or bool(set(d.r) & set(o.w))

    def emit(self, final_wait_eng="sp"):
        nc = self.nc
        ops = self.ops
        for o in ops:
            for d in o.deps:
                if self._needs_sync(d, o):
                    d.signal = True
            if o.dma:
                o.signal = True
        stack = contextlib.ExitStack()
        eng_sems, eng_cnt, eng_epoch = {}, {}, {}

        def new_sem(name):
            return stack.enter_context(nc.semaphore(name))

        for e in COMPUTE:
            eng_epoch[e] = 0
            eng_cnt[e] = 0
            eng_sems[e] = new_sem(f"c_{e}_0")
        dma_sems = [new_sem(f"d_{i}") for i in range(self.n_dma_sems)]
        dma_cnt = [0] * self.n_dma_sems
        dma_last = [None] * self.n_dma_sems
        dma_rr = 0
        for o in ops:
            if not o.signal:
                continue
            if o.dma and o.eng == "pool":
                o.tok = (new_sem(f"sw_{o.idx}"), 16)
            elif o.dma:
                i = dma_rr % self.n_dma_sems
                dma_rr += 1
                if dma_last[i] is not None:
                    o.deps.add(dma_last[i])
                dma_cnt[i] += 16
                o.tok = (dma_sems[i], dma_cnt[i])
                dma_last[i] = o
            else:
                e = o.eng
                if eng_cnt[e] >= EPOCH:
                    eng_epoch[e] += 1
                    eng_cnt[e] = 0
                    eng_sems[e] = new_sem(f"c_{e}_{eng_epoch[e]}")
                eng_cnt[e] += 1
                o.tok = (eng_sems[e], eng_cnt[e])
        streams = {}
        for o in ops:
            streams.setdefault(o.eng, []).append(o)
        all_dma = [o for o in ops if o.dma]

        def run_stream(e, eng_obj):
            seen = {}
            for o in streams.get(e, []):
                waits = {}
                for d in o.deps:
                    if not self._needs_sync(d, o):
                        continue
                    sem, val = d.tok
                    key = id(sem)
                    if seen.get(key, 0) >= val:
                        continue
                    if key not in waits or waits[key][1] < val:
                        waits[key] = (sem, val)
                for key, (sem, val) in waits.items():
                    eng_obj.wait_ge(sem, val)
                    seen[key] = val
                meth, (cargs, ckw) = o.fn
                ins = getattr(eng_obj, meth)(*cargs, **ckw)
                if o.signal:
                    assert ins is not None, f"op {o.name} returned no instruction"
                    ins.then_inc(o.tok[0], 16 if o.dma else 1)
            if e == final_wait_eng:
                fin = {}
                for d in all_dma:
                    sem, val = d.tok
                    if fin.get(id(sem), (None, 0))[1] < val:
                        fin[id(sem)] = (sem, val)
                for key, (sem, val) in fin.items():
                    if seen.get(key, 0) < val:
                        eng_obj.wait_ge(sem, val)
                for ce in COMPUTE:
                    last = None
                    for o in streams.get(ce, []):
                        if o.signal:
                            last = o
                    if last is not None:
                        sem, val = last.tok
                        if seen.get(id(sem), 0) < val:
                            eng_obj.wait_ge(sem, val)

        with stack:
            with nc.Block() as block:
                @block.sync
                def _(eng):
                    run_stream("sp", eng)

                @block.tensor
                def _(eng):
                    run_stream("pe", eng)

                @block.scalar
                def _(eng):
                    run_stream("act", eng)

                @block.vector
                def _(eng):
                    run_stream("dve", eng)

                @block.gpsimd
                def _(eng):
                    run_stream("pool", eng)
        return {e: len(v) for e, v in streams.items()}


IN_SHAPES = {
    "x_p": [SEQ, D], "x_s": [NS, D], "st_conv": [2, NS, 3, 512], "st_h": [2, NS, 512],
    "st_pool": [2, NS, 15, D], "norm_mix": [4, D], "norm_ffn": [4, D], "norm_final": [D],
    "w_in": [2, D, 2048], "w_out": [2, D, D], "v_norm": [2, 512], "sgu_w": [2, 8, 128, 128],
    "sgu_b": [2, 8, 128], "conv_w": [2, 4, 512], "conv_b": [2, 512], "gate_a_w": [2, 8, 64, 64],
    "gate_a_b": [2, 512], "gate_x_w": [2, 8, 64, 64], "gate_x_b": [2, 512], "lru_lambda": [2, 512],
    "pool_w": [2, 4, 256, 256], "pool_b": [2, D], "pool_scale": [2, D],
    "ffn_w1": [4, D, 4096], "ffn_w2": [4, 4096, D],
}
OUT_SHAPES = {
    "y_p": [SEQ, D], "y_s": [NS, D], "sgu_v": [2, NS, 512], "conv_p": [2, 3, 512],
    "conv_s": [2, NS, 3, 512], "h_p": [2, 512], "h_s": [2, NS, 512],
    "pool_p": [2, 15, D], "pool_s": [2, NS, 15, D],
}

MIX0, FFN0, FIN0, CW0, CB0, GAB0, GXB0, LAM0, PB0, PS0 = 0, 32, 64, 72, 104, 112, 120, 128, 136, 152
DC0, DHC0, DHBA0, DHBX0, DPBS0 = 0, 8, 16, 24, 32


def build():
    nc = bass.Bass("TRN2", target_bir_lowering=False)
    I = {k: nc.dram_tensor(k, s, F32, kind="ExternalInput").ap() for k, s in IN_SHAPES.items()}
    O = {k: nc.dram_tensor(k, s, F32, kind="ExternalOutput").ap() for k, s in OUT_SHAPES.items()}
    st = contextlib.ExitStack()
    with st:
        def sb(name, shape, dt):
            return st.enter_context(nc.sbuf_tensor(name, shape, dt))

        X = sb("X", [128, 8, NT], F32)
        A = sb("A", [128, 16512], BF16)
        B = sb("B", [128, 16512], BF16)
        RING = [sb(f"ring{i}", [128, 8, 512], BF16) for i in range(4)]
        SQ = sb("SQ", [128, 8, 512], BF16)
        T = [sb(f"T{i}", [128, 512], F32) for i in range(6)]
        IDENT = sb("IDENT", [128, 128], F32)
        ONES = sb("ONES", [128, 128], BF16)
        VECT = sb("VECT", [128, 168], F32)
        DV = sb("DV", [128, 48], F32)
        CONSTS = sb("CONSTS", [128, 4], F32)
        INVCNT = sb("INVCNT", [128, 16], F32)
        WCNT = sb("WCNT", [128, 2, 4], F32)
        SEL = sb("SEL", [128, 4, 8], F32)
        WMT = sb("WMT", [128, 8, 128], BF16)
        WR = sb("WR", [128, 1024], F32)
        WRAW = WR[:, :].rearrange("p (h s) -> p h s", h=8)
        NTS = [sb(f"NT{i}", [128, 512], F32) for i in range(3)]
        XSS_ = WR
        BH = sb("BH", [128, 4, 128], F32)
        W0H = sb("W0H", [128, 4], F32)
        VNB = sb("VNB", [128, 512], F32)
        WBD = sb("WBD", [128, 2, 4, 128], BF16)
        PW = sb("PW", [128, 4, 2, 256], BF16)
        HCAR = sb("HCAR", [128, 4], F32)
        XBS = sb("XBS", [128, 4, 16], F32)
        HSS = sb("HSS", [128, 4, 16], F32)
        HIST = sb("HIST", [128, 12, 16], F32)
        H0 = sb("H0", [128, 4, 16], F32)
        SMALL = sb("SMALL", [128, 8], F32)
        VFM = sb("VFM", [128, 64], F32)
        DUMMY = sb("DUMMY", [128, 64], F32)
        PS = [st.enter_context(nc.psum_tensor(f"ps{i}", [128, 512], F32)) for i in range(8)]

        S = Sched(nc)
        bank_rr = [0]
        fence_n = [0]

        bank_free = list(range(8))

        def bank():
            assert bank_free, "no free PSUM bank"
            i = bank_free.pop(0)
            return PS[i], ("P", i)

        def rel(pk):
            assert pk[1] not in bank_free
            bank_free.append(pk[1])

        def fence_region(rg):
            i = fence_n[0] % 64
            fence_n[0] += 1
            S.fence(rg, "memset", C(DUMMY[0:1, i:i + 1], 0.0))

        def fence_all():
            for rg in REGIONS:
                fence_region(rg)

        def even_norm(l, tn):
            nn = TILES[tn][1]
            norm_tile(tn, MIX0 + l * 8, lambda c: (XNT[:, c, 0:nn], ("A", "XNT", c)))

        def odd_norm(l, tn):
            nn = TILES[tn][1]
            if tn < 4:
                norm_tile(tn, MIX0 + l * 8, lambda c: (XNF[:, c, 15:15 + nn], ("A", "XNF", c)))
            else:
                norm_tile(tn, MIX0 + l * 8, lambda c: (XNS[:, c, :], ("A", "XNS", c)))

        def mixer_begin(l):
            fence_region("A")
            if l % 2 == 0:
                even_norm(l, 0)
            else:
                S.pool("memset", C(XNF[:, :, 0:15], 0.0), w=[("A", "XNF", c) for c in range(8)])
                odd_norm(l, 0)

        def a_bf(off, n):
            return A[:, off:off + n]

        def a_f32(off, n):
            return A[:, off:off + 2 * n].bitcast(F32)

        def b_bf(off, n):
            return B[:, off:off + n]

        def b_f32(off, n):
            return B[:, off:off + 2 * n].bitcast(F32)

        XN = A[:, 0:8 * NT].rearrange("p (c n) -> p c n", c=8)
        HT = B[:, 0:8 * NT].rearrange("p (c n) -> p c n", c=8)
        XNT = a_bf(0, 4096).rearrange("p (c n) -> p c n", c=8)
        VTM = a_bf(4096, 2048).rearrange("p (c n) -> p c n", c=4)
        CAT = a_bf(6144, 4096).rearrange("p (c n) -> p c n", c=8)
        G2 = [a_f32(10240, 512), a_f32(11264, 512)]
        AT = [a_f32(12288 + 1024 * i, 512) for i in range(4)]
        WO = [b_bf(0, 4096).rearrange("p (c n) -> p c n", c=8), b_bf(4096, 4096).rearrange("p (c n) -> p c n", c=8)]
        XBUF = b_f32(8192, 4 * 516).rearrange("p (c n) -> p c n", c=4)
        BT = [b_f32(8192 + 2 * 4 * 516 + 1024 * i, 512) for i in range(3)]
        XNF = a_f32(0, 8 * 528).rearrange("p (c n) -> p c n", c=8)
        SW = [a_f32(2 * 8 * 528 + 2 * 528 * i, 528) for i in range(4)]
        XNS = a_f32(2 * 8 * 528 + 8 * 528, 128).rearrange("p (c n) -> p c n", c=8)
        PBUF = b_bf(0, 4096).rearrange("p (c n) -> p c n", c=8)
        SPT = [b_f32(4096, 1024), b_f32(4096 + 2048, 1024)]
        PWF = b_f32(8192, 2048).rearrange("p (g k n) -> p g k n", g=4, k=2)
        PSB = b_f32(12288, 1024)
        XS = [b_f32(0, 4096).rearrange("p (c n) -> p c n", c=4), b_f32(8192, 4096).rearrange("p (c n) -> p c n", c=4)]
        XSS = None
        YF = a_f32(0, 4096).rearrange("p (c n) -> p c n", c=8)
        STG = [a_f32(8192, 512), a_f32(9216, 512), a_f32(10240, 512), a_f32(11264, 512)]

        EPS1 = CONSTS[:, 0:1]
        EPS4 = CONSTS[:, 1:2]

        S.pool("memset", C(IDENT[:], 1.0), w=["IDENT"])
        S.pool("affine_select", C(out=IDENT[:], in_=IDENT[:], pattern=[[-1, 128]], compare_op=ALU.is_equal,
                                         fill=0.0, base=0, channel_multiplier=1), r=["IDENT"], w=["IDENT"])
        S.pool("memset", C(ONES[:], 1.0), w=["ONES"])
        S.pool("memset", C(CONSTS[:, 0:1], EPS), w=["CONSTS"])
        S.pool("memset", C(CONSTS[:, 1:2], 4 * EPS), r=["CONSTS"], w=["CONSTS"])
        S.pool("memset", C(CONSTS[:, 2:3], -0.5), r=["CONSTS"], w=["CONSTS"])
        S.pool("memset", C(CONSTS[:, 3:4], 1e-18), r=["CONSTS"], w=["CONSTS"])
        for j in range(16):
            S.pool("memset", C(INVCNT[:, j:j + 1], 1.0 / (j + 1)), r=["INVCNT"] if j else [], w=["INVCNT"])
        for g in range(2):
            w = 2 << g
            for j in range(w - 1):
                S.pool("memset", C(WCNT[:, g, j:j + 1], float(w) / (j + 1)), r=["WCNT"] if (g or j) else [], w=["WCNT"])
        S.pool("memset", C(SEL[:], 1.0), w=["SEL"])
        for g in range(4):
            w = 2 << g
            S.pool("affine_select", C(out=SEL[0:120, g, :], in_=SEL[0:120, g, :], pattern=[[-15, 8]],
                                                       compare_op=ALU.is_ge, fill=0.0, base=-(16 - w),
                                                       channel_multiplier=1), r=["SEL"], w=["SEL"])
            S.pool("affine_select", C(out=SEL[0:120, g, :], in_=SEL[0:120, g, :], pattern=[[15, 8]],
                                                  compare_op=ALU.is_ge, fill=0.0, base=14,
                                                  channel_multiplier=-1), r=["SEL"], w=["SEL"])
        for t in range(2):
            S.dma("dma_start", C(out=XS[t], in_=I["x_p"][t * 512:(t + 1) * 512, :].rearrange("(tb p) d -> p tb d", p=128)),
                  w=[("B", "XS", t)])
        VRA, VRB = T[0], T[1]
        srcs_a = [
            (I["norm_mix"].rearrange("l (c p) -> (l c) p", p=128), 0, 32),
            (I["norm_ffn"].rearrange("l (c p) -> (l c) p", p=128), 32, 32),
            (I["norm_final"].rearrange("(c p) -> c p", p=128), 64, 8),
            (I["conv_w"].rearrange("e k (c p) -> (e k c) p", p=128), 72, 32),
            (I["conv_b"].rearrange("e (c p) -> (e c) p", p=128), 104, 8),
            (I["gate_a_b"].rearrange("e (c p) -> (e c) p", p=128), 112, 8),
            (I["gate_x_b"].rearrange("e (c p) -> (e c) p", p=128), 120, 8),
        ]
        for src, r0, n in srcs_a:
            S.dma("dma_start", C(out=VRA[r0:r0 + n, 0:128], in_=src), w=[("T", 0)])
        srcs_b = [
            (I["lru_lambda"].rearrange("e (c p) -> (e c) p", p=128), 0, 8),
            (I["pool_b"].rearrange("o (c p) -> (o c) p", p=128), 8, 16),
            (I["pool_scale"].rearrange("o (c p) -> (o c) p", p=128), 24, 16),
        ]
        for src, r0, n in srcs_b:
            S.dma("dma_start", C(out=VRB[r0:r0 + n, 0:128], in_=src), w=[("T", 1)])
        pb, pk = bank()
        S.pe("transpose", C(out=pb[:, 0:128], in_=VRA[:, 0:128], identity=IDENT[:]), r=[("T", 0), "IDENT"], w=[pk])
        S.pe("transpose", C(out=pb[:, 128:168], in_=VRB[0:40, 0:128], identity=IDENT[0:40, 0:40]),
             r=[("T", 1), "IDENT"], w=[pk])
        S.dve("tensor_copy", C(out=VECT[:], in_=pb[:, 0:168]), r=[pk], w=["VECT"])
        rel(pk)
        S.act("activation", C(out=SMALL[:, 0:8], in_=VECT[:, LAM0:LAM0 + 8], func=AF.Exp, scale=-1.0),
              r=["VECT"], w=["SMALL"])
        S.act("activation", C(out=SMALL[:, 0:8], in_=SMALL[:, 0:8], func=AF.Ln, bias=1.0), r=["SMALL"], w=["SMALL"])
        S.dve("tensor_scalar", C(out=DV[:, DC0:DC0 + 8], in0=SMALL[:, 0:8], scalar1=-8.0, scalar2=None,
                                        op0=ALU.mult), r=["SMALL"], w=["DV"])
        S.dve("tensor_scalar", C(out=DV[:, DHC0:DHC0 + 8], in0=SMALL[:, 0:8], scalar1=-4.0, scalar2=None,
                                        op0=ALU.mult), r=["SMALL", "DV"], w=["DV"])
        S.dve("tensor_scalar", C(out=DV[:, DHBA0:DHBA0 + 16], in0=VECT[:, GAB0:GAB0 + 16], scalar1=0.5, scalar2=None,
                                        op0=ALU.mult), r=["VECT", "DV"], w=["DV"])
        S.dve("tensor_tensor", C(out=DV[:, DPBS0:DPBS0 + 16], in0=VECT[:, PB0:PB0 + 16], in1=VECT[:, PS0:PS0 + 16],
                                        op=ALU.mult), r=["VECT", "DV"], w=["DV"])
        units = []
        for l in range(4):
            if l % 2 == 0:
                wi = I["w_in"][l // 2].rearrange("(k p) n -> p k n", p=128)
                for blk in (1, 2, 3, 0):
                    units.append(wi[:, :, blk * 512:(blk + 1) * 512])
            w1 = I["ffn_w1"][l].rearrange("(k p) n -> p k n", p=128)
            w2 = I["ffn_w2"][l].rearrange("(k p) n -> p k n", p=128)
            for q in range(4):
                for h in range(2):
                    units.append(w1[:, :, q * 1024 + h * 512:q * 1024 + (h + 1) * 512])
                for h in range(2):
                    units.append(w2[:, q * 8:(q + 1) * 8, h * 512:(h + 1) * 512])
        ws = {"free": [0, 1, 2, 3], "next": 0, "loaded": {}, "cur": 0}

        def ws_issue():
            while ws["free"] and ws["next"] < len(units):
                slot = ws["free"].pop(0)
                u = ws["next"]
                ws["next"] += 1
                S.dma("dma_start", C(out=RING[slot][:], in_=units[u]), w=[("R", slot)], q="pool")
                ws["loaded"][u] = slot

        def ws_acquire():
            u = ws["cur"]
            ws["cur"] += 1
            assert u in ws["loaded"], "weight unit not issued"
            return u, ws["loaded"][u]

        def ws_release(u):
            slot = ws["loaded"].pop(u)
            ws["free"].append(slot)
            ws_issue()

        ws_issue()

        cp_rr = [0]

        def evac_copy(out_ap, in_ap, r, w):
            if cp_rr[0] % 2 == 0:
                S.act("activation", C(out=out_ap, in_=in_ap, func=AF.Copy), r=r, w=w)
            else:
                S.dve("tensor_copy", C(out=out_ap, in_=in_ap), r=r, w=w)
            cp_rr[0] += 1

        def load_x_tile(t):
            xs = XS[t % 2]
            xk = ("B", "XS", t % 2)
            if t >= 2:
                S.dma("dma_start", C(out=xs, in_=I["x_p"][t * 512:(t + 1) * 512, :].rearrange(
                    "(tb p) d -> p tb d", p=128)), w=[xk])
            for c in range(8):
                pb, pk = bank()
                for tb in range(4):
                    S.pe("transpose", C(out=pb[:, tb * 128:(tb + 1) * 128], in_=xs[:, tb, c * 128:(c + 1) * 128],
                                        identity=IDENT[:]), r=[xk, "IDENT"], w=[pk])
                evac_copy(X[:, c, t * 512:(t + 1) * 512], pb[:, :], [pk], [("X", c, t)])
                rel(pk)

        def load_x_sample():
            S.dma("dma_start", C(out=XSS_[0:NS, :], in_=I["x_s"]), w=[("WR", 0), ("WR", 1)])
            pb, pk = bank()
            for c in range(8):
                S.pe("transpose", C(out=pb[:, c * 16:(c + 1) * 16], in_=XSS_[0:NS, c * 128:(c + 1) * 128],
                                    identity=IDENT[0:NS, 0:NS]), r=[("WR", 0), ("WR", 1), "IDENT"], w=[pk])
            S.dve("tensor_copy", C(out=X[:, :, SEQ:NT], in_=pb[:, 0:128].rearrange("p (c n) -> p c n", c=8)),
                  r=[pk], w=[("X", c, 4) for c in range(8)])
            rel(pk)
            for e_ in range(2):
                S.dma("dma_start", C(out=O["conv_s"][e_, :, 0:2, :], in_=I["st_conv"][e_, :, 1:3, :]))
                S.dma("dma_start", C(out=O["pool_s"][e_, :, 0:14, :], in_=I["st_pool"][e_, :, 1:15, :]))

        def norm_tile(t, gcol, out_fn, split_sq=False):
            c0, N = TILES[t]
            pb, pk = bank()
            for c in range(8):
                s_ = c % 6
                S.act("activation", C(out=SQ[:, s_, 0:N], in_=X[:, c, c0:c0 + N], func=AF.Square),
                      r=[("X", c, t)], w=[("SQ", s_)])
                S.pe("matmul", C(out=pb[:, 0:N], lhsT=ONES[:], rhs=SQ[:, s_, 0:N], start=(c == 0), stop=(c == 7)),
                     r=[("SQ", s_), "ONES"], w=[pk])
            rs = T[5]
            S.act("activation", C(out=rs[:, 0:N], in_=pb[:, 0:N], func=AF.Ln, bias=EPS1, scale=1.0 / D),
                  r=[pk, "CONSTS"], w=[("T", 5)])
            rel(pk)
            S.act("activation", C(out=rs[:, 0:N], in_=rs[:, 0:N], func=AF.Exp, scale=-0.5), r=[("T", 5)], w=[("T", 5)])
            for c in range(8):
                oap, okey = out_fn(c)
                S.dve("scalar_tensor_tensor", C(out=oap, in0=X[:, c, c0:c0 + N], scalar=VECT[:, gcol + c:gcol + c + 1],
                                                in1=rs[:, 0:N], op0=ALU.mult, op1=ALU.mult),
                      r=[("X", c, t), ("T", 5), "VECT"], w=[okey])

        def gelu2(pbap, pk, out_ap, okey, tmp, tk, P, N):
            tt = tmp[0:P, 0:N]
            S.act("activation", C(out=tt, in_=pbap, func=AF.Square, scale=0.044715 ** 0.5), r=[pk], w=[tk])
            yield
            S.dve("scalar_tensor_tensor", C(out=tt, in0=tt, scalar=1.0, in1=pbap, op0=ALU.add, op1=ALU.mult), r=[tk, pk], w=[tk])
            yield
            S.act("activation", C(out=tt, in_=tt, func=AF.Tanh, scale=0.7978845608028654), r=[tk], w=[tk])
            yield
            S.dve("scalar_tensor_tensor", C(out=out_ap, in0=tt, scalar=1.0, in1=pbap, op0=ALU.add, op1=ALU.mult),
                  r=[tk, pk], w=[okey])
            yield

        def run_multi(groups):
            state = [{"it": iter(ch), "k": k, "active": []} for ch, k in groups]
            while True:
                progressed = False
                for st_ in state:
                    while len(st_["active"]) < st_["k"]:
                        nxt = next(st_["it"], None)
                        if nxt is None:
                            break
                        st_["active"].append(nxt)
                    for g in list(st_["active"]):
                        try:
                            next(g)
                            progressed = True
                        except StopIteration:
                            st_["active"].remove(g)
                            progressed = True
                if not progressed:
                    break

        stg_rr = [0]

        def out_T(srcs, skeys, n, dst_fn):
            for g in range(len(srcs) // 4):
                pb, pk = bank()
                for j in range(4):
                    src = srcs[g * 4 + j]
                    S.pe("transpose", C(out=pb[0:n, j * 128:(j + 1) * 128], in_=src,
                                                                  identity=IDENT[:]), r=[skeys[g * 4 + j], "IDENT"], w=[pk])
                si = stg_rr[0] % 4
                stg_rr[0] += 1
                stg, sk = STG_CUR[0][si], STG_KEYS[0][si]
                evac_copy(stg[0:n, :], pb[0:n, :], [pk], [sk])
                rel(pk)
                S.dma("dma_start", C(out=dst_fn(g), in_=stg[0:n, :]), r=[sk])

        STG_CUR = [[T[0], T[1], T[2], T[3]]]
        STG_KEYS = [[("T", 0), ("T", 1), ("T", 2), ("T", 3)]]

        wbd_keys = [("WBDl", mi, hh) for mi in range(2) for hh in range(2)]

        def even_setup_wmt_pe():
            for h in range(8):
                pb, pk = bank()
                S.pe("transpose", C(out=pb[:, 0:128], in_=WRAW[:, h, :], identity=IDENT[:]),
                     r=[("WR", 0), ("WR", 1), "IDENT"], w=[pk])
                evac_copy(WMT[:, h, :], pb[:, 0:128], [pk], [("WMT", h)])
                rel(pk)

        def even_setup_consts(l, part="all"):
            e_ = l // 2
            S.dma("dma_start", C(out=WRAW[:], in_=I["sgu_w"][e_].rearrange("h t s -> t h s")), w=[("WR", 0), ("WR", 1)])
            S.pool("affine_select", C(out=WRAW[:], in_=WRAW[:], pattern=[[0, 8], [-1, 128]], compare_op=ALU.is_ge,
                                             fill=0.0, base=0, channel_multiplier=1), r=[("WR", 0), ("WR", 1)], w=[("WR", 0), ("WR", 1)])
            if part == "all":
                even_setup_wmt_pe()
            gi = fence_n[0] % 64
            fence_n[0] += 1
            S.pool("memset", C(DUMMY[0:1, gi:gi + 1], 0.0), w=["BH", "W0H"] + [("BHp", h) for h in range(8)] + [("W0p", h) for h in range(8)])
            for h in range(8):
                S.dma("dma_start", C(out=BH[(h % 2) * 64:(h % 2) * 64 + 64, h // 2, :],
                                     in_=I["sgu_b"][e_, h, :].partition_broadcast(64)), r=["BH"], w=[("BHp", h)], q="act")
                S.dma("dma_start", C(out=W0H[(h % 2) * 64:(h % 2) * 64 + 64, h // 2:h // 2 + 1],
                                     in_=I["sgu_w"][e_, h, 0:1, 0:1].rearrange("a b -> (a b)").partition_broadcast(64)),
                      r=["W0H"], w=[("W0p", h)], q="act")
            S.pool("tensor_scalar", C(out=BH[:], in0=BH[:], scalar1=0.5, scalar2=None, op0=ALU.mult),
                   r=[("BHp", h) for h in range(8)], w=["BH"] + [("BHp", h) for h in range(8)])
            S.pool("tensor_scalar", C(out=W0H[:], in0=W0H[:], scalar1=0.5, scalar2=None, op0=ALU.mult),
                   r=[("W0p", h) for h in range(8)], w=["W0H"] + [("W0p", h) for h in range(8)])
            S.dma("dma_start", C(out=VNB[:], in_=I["v_norm"][e_].partition_broadcast(128)), w=["VNB"], q="act")
            S.pool("memset", C(WBD[:], 0.0), w=["WBD"])
            for mi, nm in enumerate(("gate_a_w", "gate_x_w")):
                for hh in range(2):
                    S.dma("dma_start", C(
                        out=WBD[hh * 64:(hh + 1) * 64, mi, :, hh * 64:(hh + 1) * 64],
                        in_=I[nm][e_, hh::2].rearrange("j i o -> i j o")), r=["WBD"], w=[("WBDl", mi, hh)], q="pool")

        def even_setup_state(l):
            e_ = l // 2
            scs = WR[:, 512:1024]
            S.dma("dma_start", C(out=scs[0:NS, :], in_=I["st_h"][e_]),
                  w=[("WR", 1)])
            pb, pk = bank()
            for j in range(4):
                S.pe("transpose", C(out=pb[:, j * 16:(j + 1) * 16], in_=scs[0:NS, j * 128:(j + 1) * 128],
                                                       identity=IDENT[0:NS, 0:NS]), r=[("WR", 1), "IDENT"], w=[pk])
            S.dve("tensor_scalar", C(out=H0[:], in0=pb[:, 0:64].rearrange("p (c n) -> p c n", c=4), scalar1=0.5, scalar2=None, op0=ALU.mult), r=[pk], w=["H0"])
            rel(pk)
            for k in range(3):
                S.dma("dma_start", C(out=scs[0:NS, :], in_=I["st_conv"][e_, :, k, :]), w=[("WR", 1)])
                pb, pk = bank()
                for j in range(4):
                    S.pe("transpose", C(out=pb[:, j * 16:(j + 1) * 16], in_=scs[0:NS, j * 128:(j + 1) * 128],
                                                           identity=IDENT[0:NS, 0:NS]), r=[("WR", 1), "IDENT"], w=[pk])
                S.dve("tensor_copy", C(out=HIST[:, k * 4:(k + 1) * 4, :],
                                                          in_=pb[:, 0:64].rearrange("p (c n) -> p c n", c=4)), r=[pk], w=[("HIST", k)])
                rel(pk)

        STATE_SCR = [(T[4], ("T", 4)), (NTS[0], ("NT", 0)), (NTS[1], ("NT", 1)), (NTS[2], ("NT", 2))]

        def even_setup_state_load(l):
            e_ = l // 2
            srcs = [I["st_h"][e_]] + [I["st_conv"][e_, :, k, :] for k in range(3)]
            for (scr, sk), src_ap in zip(STATE_SCR, srcs):
                S.dma("dma_start", C(out=scr[0:NS, :], in_=src_ap), w=[sk])

        def even_setup_state_pe(l):
            for i, (scr, sk) in enumerate(STATE_SCR):
                pb, pk = bank()
                for j in range(4):
                    S.pe("transpose", C(out=pb[:, j * 16:(j + 1) * 16], in_=scr[0:NS, j * 128:(j + 1) * 128],
                                        identity=IDENT[0:NS, 0:NS]), r=[sk, "IDENT"], w=[pk])
                if i == 0:
                    S.dve("tensor_scalar", C(out=H0[:], in0=pb[:, 0:64].rearrange("p (c n) -> p c n", c=4), scalar1=0.5,
                                             scalar2=None, op0=ALU.mult), r=[pk], w=["H0"])
                else:
                    k = i - 1
                    S.dve("tensor_copy", C(out=HIST[:, k * 4:(k + 1) * 4, :], in_=pb[:, 0:64].rearrange("p (c n) -> p c n", c=4)),
                          r=[pk], w=[("HIST", k)])
                rel(pk)

        def even_mixer(l):
            e_ = l // 2
            fence_region("B")
            wo = I["w_out"][e_].rearrange("(k p) n -> p k n", p=128)
            for h in range(2):
                S.dma("dma_start", C(out=WO[h], in_=wo[:, :, h * 512:(h + 1) * 512]), w=[("B", "WO", h)], q="pool")
            S.pool("memset", C(XBUF[:, :, 0:3], 0.0), w=[("B", "XBUF", j) for j in range(4)])
            S.pool("memset", C(HCAR[:], 0.0), w=[("HCAR", j) for j in range(4)])
            uv, sv = ws_acquire()
            ug, sg = ws_acquire()
            ux, sx = ws_acquire()
            uu, su = ws_acquire()
            Wv, Wu, Wg, Wx = RING[sv], RING[su], RING[sg], RING[sx]
            Kv, Ku, Kg, Kx = ("R", sv), ("R", su), ("R", sg), ("R", sx)
            cvec = lambda base, j: VECT[:, base + e_ * 4 + j:base + e_ * 4 + j + 1]
            dvec = lambda base, j: DV[:, base + e_ * 4 + j:base + e_ * 4 + j + 1]

            WKEYS = [("WR", 0), ("WR", 1)]
            GEL_VU = [(T[0], ("T", 0)), (T[1], ("T", 1))]
            G2U = [(G2[0], ("A", "G2", 0)), (G2[1], ("A", "G2", 1))]
            TSP = [(AT[0], ("A", "AT", 0)), (AT[1], ("A", "AT", 1))]
            GEL_L = [(T[2], ("T", 2)), (T[3], ("T", 3))]
            GG2 = [(AT[2], ("A", "AT", 2)), (AT[3], ("A", "AT", 3))]
            XCP = [(BT[0], ("B", "BT", 0)), (BT[1], ("B", "BT", 1))]
            THR = [(T[4], ("T", 4)), (BT[2], ("B", "BT", 2))]
            THI = [(NTS[0], ("NT", 0)), (NTS[1], ("NT", 1))]
            A2P = [(NTS[2], ("NT", 2)), (WR[:, 0:512], ("WR", 0))]
            XCBP = [(SQ[:, 6, :], ("SQ", 6)), (SQ[:, 7, :], ("SQ", 7))]

            NTL = 5
            n_xnt = [16, 16, 16, 16, 13]
            xnt_reads = [0] * NTL
            norm_done = [True] + [False] * (NTL - 1)
            vtm_cnt = [0] * NTL
            sgu_done = [0] * NTL
            v_done = [False] * NTL
            cat_cnt = [0] * NTL
            w_mm = [0] * NTL
            g_done = [[False] * 4 for _ in range(NTL)]
            x_done = [[False] * 4 for _ in range(NTL)]
            xnt_keys = [("A", "XNT", c) for c in range(8)]

            def n_chain(t):
                while xnt_reads[t - 1] < n_xnt[t - 1]:
                    yield
                even_norm(l, t)
                norm_done[t] = True
                yield

            def v_chain(t, tb, p):
                c0, N = TILES[t]
                smp = (t == 4)
                P = NS if smp else 128
                while not norm_done[t]:
                    yield
                while not bank_free:
                    yield
                pb, pk = bank()
                for k in range(8):
                    S.pe("matmul", C(out=pb[0:P, :], lhsT=XNT[:, k, tb * 128:tb * 128 + P], rhs=Wv[:, k, :],
                                     start=(k == 0), stop=(k == 7)), r=[xnt_keys[k], Kv], w=[pk])
                xnt_reads[t] += 1
                yield
                g2, g2k = G2U[p]
                gel, gk = GEL_VU[p]
                yield from gelu2(pb[0:P, :], pk, g2[0:P, :], g2k, gel, gk, P, 512)
                rel(pk)
                sm, smk = SMALL[0:P, p:p + 1], ("SMALL", p)
                S.act("activation", C(out=gel[0:P, :], in_=g2[0:P, :], func=AF.Square, accum_out=sm), r=[g2k], w=[gk, smk])
                yield
                S.dve("tensor_scalar", C(out=sm, in0=sm, scalar1=1.0 / 512, scalar2=4 * EPS, op0=ALU.mult, op1=ALU.add),
                      r=[smk], w=[smk])
                yield
                S.pool("tensor_tensor", C(out=sm, in0=sm, in1=CONSTS[0:P, 2:3], op=ALU.pow), r=[smk, "CONSTS"], w=[smk])
                yield
                if not smp:
                    while t > 0 and sgu_done[t - 1] < 4:
                        yield
                    S.dve("scalar_tensor_tensor", C(out=VTM[:, tb, :], in0=g2[:, :], scalar=sm, in1=VNB[:, :],
                                                    op0=ALU.mult, op1=ALU.mult), r=[g2k, smk, "VNB"], w=[("A", "VTM", tb)])
                    vtm_cnt[t] += 1
                    yield
                else:
                    vs, vk = TSP[p]
                    S.dve("scalar_tensor_tensor", C(out=vs[0:NS, :], in0=g2[0:NS, :], scalar=sm, in1=VNB[0:NS, :],
                                                    op0=ALU.mult, op1=ALU.mult), r=[g2k, smk, "VNB"], w=[vk])
                    yield
                    S.dma("dma_start", C(out=O["sgu_v"][e_], in_=vs[0:NS, :]), r=[vk])
                    while not bank_free:
                        yield
                    pbv, pkv = bank()
                    for j in range(4):
                        S.pe("transpose", C(out=pbv[:, j * 16:(j + 1) * 16], in_=vs[0:NS, j * 128:(j + 1) * 128],
                                            identity=IDENT[0:NS, 0:NS]), r=[vk, "IDENT"], w=[pkv])
                    S.dve("tensor_copy", C(out=VFM[:, :], in_=pbv[:, 0:64]), r=[pkv], w=["VFM"])
                    rel(pkv)
                    v_done[t] = True
                    yield

            def u_chain(t, j, p):
                c0, N = TILES[t]
                smp = (t == 4)
                while not norm_done[t]:
                    yield
                while not bank_free:
                    yield
                pb, pk = bank()
                for k in range(8):
                    S.pe("matmul", C(out=pb[:, 0:N], lhsT=Wu[:, k, j * 128:(j + 1) * 128], rhs=XNT[:, k, 0:N],
                                     start=(k == 0), stop=(k == 7)), r=[xnt_keys[k], Ku], w=[pk])
                xnt_reads[t] += 1
                yield
                u2, u2k = G2U[p]
                gel, gk = GEL_VU[p]
                yield from gelu2(pb[:, 0:N], pk, u2[:, 0:N], u2k, gel, gk, 128, N)
                rel(pk)
                ts_, tsk = TSP[p]
                if not smp:
                    while vtm_cnt[t] < 4:
                        yield
                    while not bank_free:
                        yield
                    pb2, pk2 = bank()
                    for tb in range(4):
                        for hh in range(2):
                            S.pe("matmul", C(out=pb2[hh * 64:(hh + 1) * 64, tb * 128:(tb + 1) * 128],
                                             lhsT=VTM[:, tb, (2 * j + hh) * 64:(2 * j + hh + 1) * 64], rhs=WMT[:, 2 * j + hh, :],
                                             start=True, stop=True), r=[("A", "VTM", tb), ("WMT", 2 * j + hh)], w=[pk2])
                    sgu_done[t] += 1
                    yield
                    S.dve("scalar_tensor_tensor", C(out=ts_[:, :].rearrange("p (a b) -> p a b", a=4),
                                                    in0=pb2[:, :].rearrange("p (a b) -> p a b", a=4), scalar=0.5,
                                                    in1=BH[:, j, :].unsqueeze(1).to_broadcast([128, 4, 128]),
                                                    op0=ALU.mult, op1=ALU.add), r=[pk2, "BH"], w=[tsk])
                    rel(pk2)
                    yield
                else:
                    while not v_done[t]:
                        yield
                    S.dve("tensor_scalar", C(out=ts_[:, 0:NS], in0=VFM[:, j * 16:(j + 1) * 16], scalar1=W0H[:, j:j + 1],
                                             scalar2=BH[:, j, 0:1], op0=ALU.mult, op1=ALU.add), r=["VFM", "W0H", "BH"], w=[tsk])
                    yield
                while t > 0 and w_mm[t - 1] < 8:
                    yield
                S.dve("tensor_tensor", C(out=CAT[:, j, 0:N], in0=ts_[:, 0:N], in1=u2[:, 0:N], op=ALU.mult),
                      r=[tsk, u2k], w=[("A", "CAT", j)])
                cat_cnt[t] += 1
                yield

            def g_chain(t, j, p):
                c0, N = TILES[t]
                while not norm_done[t]:
                    yield
                if j >= 2:
                    while not x_done[t][j - 2]:
                        yield
                elif t > 0:
                    while not x_done[t - 1][j + 2]:
                        yield
                while not bank_free:
                    yield
                pb, pk = bank()
                for k in range(8):
                    S.pe("matmul", C(out=pb[:, 0:N], lhsT=Wg[:, k, j * 128:(j + 1) * 128], rhs=XNT[:, k, 0:N],
                                     start=(k == 0), stop=(k == 7)), r=[xnt_keys[k], Kg], w=[pk])
                xnt_reads[t] += 1
                yield
                gel, gk = WR[:, 512:1024], ("WR", 1)
                gg2, ggk = GG2[p]
                yield from gelu2(pb[:, 0:N], pk, gg2[:, 0:N], ggk, gel, gk, 128, N)
                rel(pk)
                g_done[t][j] = True

            def l_chain(t, j, p):
                c0, N = TILES[t]
                smp = (t == 4)
                gg2, ggk = GG2[p]
                while not norm_done[t]:
                    yield
                while not bank_free:
                    yield
                pbx, pkx = bank()
                for k in range(8):
                    S.pe("matmul", C(out=pbx[:, 0:N], lhsT=Wx[:, k, j * 128:(j + 1) * 128], rhs=XNT[:, k, 0:N],
                                     start=(k == 0), stop=(k == 7)), r=[xnt_keys[k], Kx], w=[pkx])
                xnt_reads[t] += 1
                yield
                xc, xck = XCP[p]
                xbk = ("B", "XBUF", j)
                cws = [VECT[:, CW0 + e_ * 16 + k * 4 + j:CW0 + e_ * 16 + k * 4 + j + 1] for k in range(4)]
                if not smp:
                    S.act("activation", C(out=XBUF[:, j, 3:3 + N], in_=pbx[:, 0:N], func=AF.Copy), r=[pkx], w=[xbk])
                    S.act("activation", C(out=xc[:, 0:N], in_=pbx[:, 0:N], func=AF.Identity, bias=cvec(CB0, j), scale=cws[3]),
                          r=[pkx, "VECT"], w=[xck])
                    rel(pkx)
                    yield
                    for k in range(3):
                        S.dve("scalar_tensor_tensor", C(out=xc[:, 0:N], in0=XBUF[:, j, k:k + N], scalar=cws[k], in1=xc[:, 0:N],
                                                        op0=ALU.mult, op1=ALU.add), r=[xbk, xck, "VECT"], w=[xck])
                        yield
                    S.pool("tensor_copy", C(out=XBUF[:, j, 0:3], in_=XBUF[:, j, N:N + 3]), r=[xbk, xck], w=[xbk])
                else:
                    S.act("activation", C(out=XBS[:, j, :], in_=pbx[:, 0:N], func=AF.Copy), r=[pkx], w=[("XBS", j)])
                    rel(pkx)
                    yield
                    S.pool("tensor_scalar", C(out=xc[:, 0:N], in0=XBS[:, j, :], scalar1=cws[3], scalar2=cvec(CB0, j),
                                              op0=ALU.mult, op1=ALU.add), r=[("XBS", j), "VECT"], w=[xck])
                    yield
                    for k in range(3):
                        S.dve("scalar_tensor_tensor", C(out=xc[:, 0:N], in0=HIST[:, k * 4 + j, :], scalar=cws[k], in1=xc[:, 0:N],
                                                        op0=ALU.mult, op1=ALU.add), r=[("HIST", k), xck, "VECT"], w=[xck])
                        yield
                xcb, xcbk = XCBP[p]
                S.act("activation", C(out=xcb[:, 0:N], in_=xc[:, 0:N], func=AF.Copy), r=[xck], w=[xcbk])
                yield
                while len(bank_free) < 2:
                    yield
                pbr, pkr = bank()
                S.pe("matmul", C(out=pbr[:, 0:N], lhsT=WBD[:, 0, j, :], rhs=xcb[:, 0:N], start=True, stop=True),
                     r=[xcbk, "WBD"] + wbd_keys, w=[pkr])
                pbi, pki = bank()
                S.pe("matmul", C(out=pbi[:, 0:N], lhsT=WBD[:, 1, j, :], rhs=xcb[:, 0:N], start=True, stop=True),
                     r=[xcbk, "WBD"] + wbd_keys, w=[pki])
                yield
                thr, thrk = THR[p]
                thi, thik = THI[p]
                a_, ak = GEL_L[p]
                a2, a2k = A2P[p]
                S.act("activation", C(out=thr[:, 0:N], in_=pbr[:, 0:N], func=AF.Tanh, bias=dvec(DHBA0, j), scale=0.5),
                      r=[pkr, "DV"], w=[thrk])
                rel(pkr)
                S.act("activation", C(out=thi[:, 0:N], in_=pbi[:, 0:N], func=AF.Tanh, bias=dvec(DHBX0, j), scale=0.5),
                      r=[pki, "DV"], w=[thik])
                rel(pki)
                yield
                S.act("activation", C(out=a_[:, 0:N], in_=thr[:, 0:N], func=AF.Exp, bias=dvec(DHC0, j), scale=dvec(DHC0, j)),
                      r=[thrk, "DV"], w=[ak])
                S.act("activation", C(out=a2[:, 0:N], in_=thr[:, 0:N], func=AF.Exp, bias=dvec(DC0, j), scale=dvec(DC0, j)),
                      r=[thrk, "DV"], w=[a2k])
                S.dve("scalar_tensor_tensor", C(out=thi[:, 0:N], in0=thi[:, 0:N], scalar=1.0, in1=xc[:, 0:N],
                                                op0=ALU.add, op1=ALU.mult), r=[thik, xck], w=[thik])
                yield
                S.act("activation", C(out=a2[:, 0:N], in_=a2[:, 0:N], func=AF.Relu, bias=1.0, scale=-1.0), r=[a2k], w=[a2k])
                S.act("activation", C(out=a2[:, 0:N], in_=a2[:, 0:N], func=AF.Ln, bias=CONSTS[:, 3:4], scale=1.0),
                      r=[a2k, "CONSTS"], w=[a2k])
                yield
                S.act("activation", C(out=a2[:, 0:N], in_=a2[:, 0:N], func=AF.Exp, scale=0.5), r=[a2k], w=[a2k])
                yield
                S.dve("scalar_tensor_tensor", C(out=thi[:, 0:N], in0=thi[:, 0:N], scalar=0.25, in1=a2[:, 0:N],
                                                op0=ALU.mult, op1=ALU.mult), r=[thik, a2k], w=[thik])
                yield
                hs, hsk = THR[p]
                if not smp:
                    S.dve("tensor_tensor_scan", C(out=hs[:, 0:N], data0=a_[:, 0:N], data1=thi[:, 0:N],
                                                  initial=HCAR[:, j:j + 1], op0=ALU.mult, op1=ALU.add),
                          r=[ak, thik, ("HCAR", j)], w=[hsk])
                    yield
                    S.pool("tensor_copy", C(out=HCAR[:, j:j + 1], in_=hs[:, N - 1:N]), r=[hsk, ("HCAR", j)], w=[("HCAR", j)])
                else:
                    S.dve("tensor_tensor", C(out=hs[:, 0:N], in0=a_[:, 0:N], in1=H0[:, j, :], op=ALU.mult), r=[ak, "H0"], w=[hsk])
                    yield
                    S.dve("tensor_tensor", C(out=hs[:, 0:N], in0=hs[:, 0:N], in1=thi[:, 0:N], op=ALU.add), r=[hsk, thik], w=[hsk])
                    yield
                    S.pool("tensor_scalar", C(out=HSS[:, j, :], in0=hs[:, 0:N], scalar1=2.0, scalar2=None, op0=ALU.mult),
                           r=[hsk], w=[("HSS", j)])
                    yield
                while (t > 0 and w_mm[t - 1] < 8) or not g_done[t][j]:
                    yield
                S.dve("tensor_tensor", C(out=CAT[:, 4 + j, 0:N], in0=hs[:, 0:N], in1=gg2[:, 0:N], op=ALU.mult),
                      r=[hsk, ggk], w=[("A", "CAT", 4 + j)])
                cat_cnt[t] += 1
                x_done[t][j] = True
                yield

            def w_chain(t, m):
                c0, N = TILES[t]
                while cat_cnt[t] < 8:
                    yield
                while not bank_free:
                    yield
                pb, pk = bank()
                for k in range(8):
                    S.pe("matmul", C(out=pb[:, 0:N], lhsT=WO[m // 4][:, k, (m % 4) * 128:(m % 4 + 1) * 128],
                                     rhs=CAT[:, k, 0:N], start=(k == 0), stop=(k == 7)),
                         r=[("A", "CAT", k), ("B", "WO", m // 4)], w=[pk])
                w_mm[t] += 1
                yield
                S.dve("tensor_tensor", C(out=X[:, m, c0:c0 + N], in0=X[:, m, c0:c0 + N], in1=pb[:, 0:N], op=ALU.add),
                      r=[pk, ("X", m, t)], w=[("X", m, t)])
                rel(pk)
                yield

            def us_chain(p):
                t, N = 4, NS
                while not norm_done[t] or not v_done[t]:
                    yield
                while not bank_free:
                    yield
                pb, pk = bank()
                for j in range(4):
                    for k in range(8):
                        S.pe("matmul", C(out=pb[:, j * NS:(j + 1) * NS], lhsT=Wu[:, k, j * 128:(j + 1) * 128], rhs=XNT[:, k, 0:N],
                                         start=(k == 0), stop=(k == 7)), r=[xnt_keys[k], Ku], w=[pk])
                xnt_reads[t] += 4
                yield
                u2, u2k = G2U[p]
                gel, gk = GEL_VU[p]
                yield from gelu2(pb[:, 0:4 * NS], pk, u2[:, 0:4 * NS], u2k, gel, gk, 128, 4 * NS)
                rel(pk)
                ts_, tsk = TSP[p]
                for j in range(4):
                    S.dve("tensor_scalar", C(out=ts_[:, j * NS:(j + 1) * NS], in0=VFM[:, j * NS:(j + 1) * NS], scalar1=W0H[:, j:j + 1],
                                             scalar2=BH[:, j, 0:1], op0=ALU.mult, op1=ALU.add), r=["VFM", "W0H", "BH", tsk], w=[tsk])
                yield
                while w_mm[t - 1] < 8:
                    yield
                S.dve("tensor_tensor", C(out=CAT[:, 0:4, 0:N], in0=ts_[:, 0:4 * NS].rearrange("p (j n) -> p j n", j=4),
                                         in1=u2[:, 0:4 * NS].rearrange("p (j n) -> p j n", j=4), op=ALU.mult),
                      r=[tsk, u2k], w=[("A", "CAT", j) for j in range(4)])
                cat_cnt[t] += 4
                yield

            def gs_chain():
                t, N = 4, NS
                while not norm_done[t] or not x_done[t - 1][2]:
                    yield
                while not bank_free:
                    yield
                pb, pk = bank()
                for j in range(4):
                    for k in range(8):
                        S.pe("matmul", C(out=pb[:, j * NS:(j + 1) * NS], lhsT=Wg[:, k, j * 128:(j + 1) * 128], rhs=XNT[:, k, 0:N],
                                         start=(k == 0), stop=(k == 7)), r=[xnt_keys[k], Kg], w=[pk])
                xnt_reads[t] += 4
                yield
                gel, gk = WR[:, 512:1024], ("WR", 1)
                gg2, ggk = GG2[0]
                yield from gelu2(pb[:, 0:4 * NS], pk, gg2[:, 0:4 * NS], ggk, gel, gk, 128, 4 * NS)
                rel(pk)
                for j in range(4):
                    g_done[t][j] = True

            def ls_chain():
                t, N, W4 = 4, NS, 4 * NS
                p = 0
                gg2, ggk = GG2[0]
                while not norm_done[t]:
                    yield
                while not bank_free:
                    yield
                pbx, pkx = bank()
                for j in range(4):
                    for k in range(8):
                        S.pe("matmul", C(out=pbx[:, j * NS:(j + 1) * NS], lhsT=Wx[:, k, j * 128:(j + 1) * 128], rhs=XNT[:, k, 0:N],
                                         start=(k == 0), stop=(k == 7)), r=[xnt_keys[k], Kx], w=[pkx])
                xnt_reads[t] += 4
                yield
                xc, xck = XCP[p]
                xbs_keys = [("XBS", j) for j in range(4)]
                S.act("activation", C(out=XBS[:, :, :], in_=pbx[:, 0:W4].rearrange("p (j n) -> p j n", j=4), func=AF.Copy),
                      r=[pkx], w=xbs_keys)
                rel(pkx)
                yield
                cw = lambda k, j: VECT[:, CW0 + e_ * 16 + k * 4 + j:CW0 + e_ * 16 + k * 4 + j + 1]
                for j in range(4):
                    S.dve("tensor_scalar", C(out=xc[:, j * NS:(j + 1) * NS], in0=XBS[:, j, :], scalar1=cw(3, j), scalar2=cvec(CB0, j),
                                             op0=ALU.mult, op1=ALU.add), r=[("XBS", j), "VECT", xck], w=[xck])
                yield
                for k in range(3):
                    for j in range(4):
                        S.dve("scalar_tensor_tensor", C(out=xc[:, j * NS:(j + 1) * NS], in0=HIST[:, k * 4 + j, :], scalar=cw(k, j),
                                                        in1=xc[:, j * NS:(j + 1) * NS], op0=ALU.mult, op1=ALU.add),
                              r=[("HIST", k), xck, "VECT"], w=[xck])
                    yield
                xcb, xcbk = XCBP[p]
                S.act("activation", C(out=xcb[:, 0:W4], in_=xc[:, 0:W4], func=AF.Copy), r=[xck], w=[xcbk])
                yield
                while len(bank_free) < 2:
                    yield
                pbr, pkr = bank()
                pbi, pki = bank()
                for j in range(4):
                    S.pe("matmul", C(out=pbr[:, j * NS:(j + 1) * NS], lhsT=WBD[:, 0, j, :], rhs=xcb[:, j * NS:(j + 1) * NS],
                                     start=True, stop=True), r=[xcbk, "WBD"] + wbd_keys, w=[pkr])
                    S.pe("matmul", C(out=pbi[:, j * NS:(j + 1) * NS], lhsT=WBD[:, 1, j, :], rhs=xcb[:, j * NS:(j + 1) * NS],
                                     start=True, stop=True), r=[xcbk, "WBD"] + wbd_keys, w=[pki])
                yield
                thr, thrk = THR[p]
                thi, thik = THI[p]
                a_, ak = GEL_L[p]
                a2, a2k = A2P[p]
                for j in range(4):
                    sl = slice(j * NS, (j + 1) * NS)
                    S.act("activation", C(out=thr[:, sl], in_=pbr[:, sl], func=AF.Tanh, bias=dvec(DHBA0, j), scale=0.5),
                          r=[pkr, "DV", thrk], w=[thrk])
                    S.act("activation", C(out=thi[:, sl], in_=pbi[:, sl], func=AF.Tanh, bias=dvec(DHBX0, j), scale=0.5),
                          r=[pki, "DV", thik], w=[thik])
                rel(pkr)
                rel(pki)
                yield
                for j in range(4):
                    sl = slice(j * NS, (j + 1) * NS)
                    S.act("activation", C(out=a_[:, sl], in_=thr[:, sl], func=AF.Exp, bias=dvec(DHC0, j), scale=dvec(DHC0, j)),
                          r=[thrk, "DV", ak], w=[ak])
                    S.act("activation", C(out=a2[:, sl], in_=thr[:, sl], func=AF.Exp, bias=dvec(DC0, j), scale=dvec(DC0, j)),
                          r=[thrk, "DV", a2k], w=[a2k])
                S.dve("scalar_tensor_tensor", C(out=thi[:, 0:W4], in0=thi[:, 0:W4], scalar=1.0, in1=xc[:, 0:W4],
                                                op0=ALU.add, op1=ALU.mult), r=[thik, xck], w=[thik])
                yield
                S.act("activation", C(out=a2[:, 0:W4], in_=a2[:, 0:W4], func=AF.Relu, bias=1.0, scale=-1.0), r=[a2k], w=[a2k])
                S.act("activation", C(out=a2[:, 0:W4], in_=a2[:, 0:W4], func=AF.Ln, bias=CONSTS[:, 3:4], scale=1.0),
                      r=[a2k, "CONSTS"], w=[a2k])
                yield
                S.act("activation", C(out=a2[:, 0:W4], in_=a2[:, 0:W4], func=AF.Exp, scale=0.5), r=[a2k], w=[a2k])
                yield
                S.dve("scalar_tensor_tensor", C(out=thi[:, 0:W4], in0=thi[:, 0:W4], scalar=0.25, in1=a2[:, 0:W4],
                                                op0=ALU.mult, op1=ALU.mult), r=[thik, a2k], w=[thik])
                yield
                hs, hsk = THR[p]
                S.dve("tensor_tensor", C(out=hs[:, 0:W4], in0=a_[:, 0:W4], in1=H0[:, :, :].rearrange("p j n -> p (j n)"), op=ALU.mult),
                      r=[ak, "H0"], w=[hsk])
                yield
                S.dve("tensor_tensor", C(out=hs[:, 0:W4], in0=hs[:, 0:W4], in1=thi[:, 0:W4], op=ALU.add), r=[hsk, thik], w=[hsk])
                yield
                S.pool("tensor_scalar", C(out=HSS[:, :, :], in0=hs[:, 0:W4].rearrange("p (j n) -> p j n", j=4), scalar1=2.0, scalar2=None,
                                          op0=ALU.mult), r=[hsk], w=[("HSS", j) for j in range(4)])
                while w_mm[t - 1] < 8 or not g_done[t][0]:
                    yield
                S.dve("tensor_tensor", C(out=CAT[:, 4:8, 0:N], in0=hs[:, 0:W4].rearrange("p (j n) -> p j n", j=4),
                                         in1=gg2[:, 0:W4].rearrange("p (j n) -> p j n", j=4), op=ALU.mult),
                      r=[hsk, ggk], w=[("A", "CAT", 4 + j) for j in range(4)])
                cat_cnt[t] += 4
                for j in range(4):
                    x_done[t][j] = True
                yield

            nc_, vu_, gc_, lc_, wc_ = [], [], [], [], []
            vu_i = 0
            for t in range(NTL):
                if t > 0:
                    nc_.append(n_chain(t))
                if t < 4:
                    for tb in range(4):
                        vu_.append(v_chain(t, tb, vu_i % 2))
                        vu_i += 1
                else:
                    vu_.append(v_chain(t, 0, vu_i % 2))
                    vu_i += 1
                if t < 4:
                    for j in range(4):
                        vu_.append(u_chain(t, j, vu_i % 2))
                        vu_i += 1
                    for j in range(4):
                        gc_.append(g_chain(t, j, j % 2))
                        lc_.append(l_chain(t, j, j % 2))
                else:
                    vu_.append(us_chain(vu_i % 2))
                    vu_i += 1
                    gc_.append(gs_chain())
                    lc_.append(ls_chain())
                for m in range(8):
                    wc_.append(w_chain(t, m))
            run_multi([(wc_, 2), (nc_, 1), (vu_, 2), (gc_, 1), (lc_, 2)])
            for u in (uv, ug, ux, uu):
                ws_release(u)
            out_T([XBUF[:, j, 0:3] for j in range(4)], [("B", "XBUF", j) for j in range(4)], 3,
                  lambda g: O["conv_p"][e_])
            S.dve("tensor_scalar", C(out=SMALL[:, 4:8], in0=HCAR[:, 0:4], scalar1=2.0, scalar2=None, op0=ALU.mult),
                  r=[("HCAR", j) for j in range(4)], w=["HOUT"])
            out_T([SMALL[:, 4 + j:5 + j] for j in range(4)], ["HOUT"] * 4, 1,
                  lambda g: O["h_p"][e_:e_ + 1, :])
            out_T([XBS[:, j, :] for j in range(4)], [("XBS", j) for j in range(4)], NS,
                  lambda g: O["conv_s"][e_, :, 2, :])
            out_T([HSS[:, j, :] for j in range(4)], [("HSS", j) for j in range(4)], NS,
                  lambda g: O["h_s"][e_])

        def odd_mixer(l):
            o_ = l // 2
            fence_region("B")
            S.dma("dma_start", C(out=PWF, in_=I["pool_w"][o_].rearrange("g (ki p) n -> p g ki n", p=128)), w=[("B", "PWF")])
            S.dma("dma_start", C(out=PSB, in_=I["pool_scale"][o_].partition_broadcast(128)), w=[("B", "PSB")])
            def fold_scale():
                for g in range(4):
                    S.dve("scalar_tensor_tensor", C(out=PW[:, g, :, :], in0=PWF[:, g, :, :], scalar=(0.5 if g == 0 else 1.0),
                                                    in1=PSB[:, g * 256:(g + 1) * 256].unsqueeze(1).to_broadcast([128, 2, 256]),
                                                    op0=ALU.mult, op1=ALU.mult),
                          r=[("B", "PWF"), ("B", "PSB")], w=[("PW", g)])
            for h in range(2):
                S.dma("dma_start", C(out=SPT[h][0:120, :], in_=I["st_pool"][o_, h * 8:(h + 1) * 8].rearrange("b j d -> (b j) d")),
                      w=[("B", "SPT", h)])
            pvec = lambda base, c: VECT[:, base + o_ * 8 + c:base + o_ * 8 + c + 1]

            def do_norm(tn):
                odd_norm(l, tn)

            for t, (c0, N) in enumerate(TILES):
                smp = (t == 4)
                for c in range(8):
                    g = c // 2
                    w = 2 << g
                    if not smp and g == 0:
                        S.dve("tensor_tensor", C(out=PBUF[:, c, 0:N], in0=XNF[:, c, 14:14 + N], in1=XNF[:, c, 15:15 + N],
                                                 op=ALU.subtract), r=[("A", "XNF", c)], w=[("B", "PBUF", c)])
                        if t == 0:
                            S.pool("memset", C(PBUF[:, c, 0:1], 0.0), r=[("B", "PBUF", c)], w=[("B", "PBUF", c)])
                    elif not smp:
                        L = 15 + N
                        cur, curk = XNF[:, c, :], ("A", "XNF", c)
                        off = 0
                        on_pool = False
                        eng = S.pool if on_pool else S.dve
                        base = 2 if on_pool else 0
                        for step in range(g + 1):
                            sh = 1 << step
                            nxt, nk = SW[base + step % 2], ("A", "SW", base + step % 2)
                            eng("tensor_tensor", C(out=nxt[:, off + sh:L], in0=cur[:, off + sh:L], in1=cur[:, off:L - sh], op=ALU.add),
                                r=[curk], w=[nk])
                            cur, curk = nxt, nk
                            off += sh
                        if on_pool:
                            S.pool("tensor_scalar", C(out=cur[:, 15:15 + N], in0=cur[:, 15:15 + N], scalar1=1.0 / w, scalar2=None,
                                                      op0=ALU.mult), r=[curk], w=[curk])
                            if t == 0:
                                S.pool("tensor_tensor", C(out=cur[:, 15:15 + w - 1], in0=cur[:, 15:15 + w - 1], in1=WCNT[:, g, 0:w - 1],
                                                          op=ALU.mult), r=[curk, "WCNT"], w=[curk])
                            S.pool("tensor_tensor", C(out=PBUF[:, c, 0:N], in0=cur[:, 15:15 + N], in1=XNF[:, c, 15:15 + N],
                                                      op=ALU.subtract), r=[curk, ("A", "XNF", c)], w=[("B", "PBUF", c)])
                        else:
                            S.dve("scalar_tensor_tensor", C(out=PBUF[:, c, 0:N], in0=cur[:, 15:15 + N], scalar=1.0 / w,
                                                            in1=XNF[:, c, 15:15 + N], op0=ALU.mult, op1=ALU.subtract),
                                  r=[curk, ("A", "XNF", c)], w=[("B", "PBUF", c)])
                            if t == 0:
                                tf, tfk = T[4], ("T", 4)
                                S.dve("tensor_tensor", C(out=tf[:, 0:w - 1], in0=cur[:, 15:15 + w - 1], in1=INVCNT[:, 0:w - 1], op=ALU.mult),
                                      r=[curk, "INVCNT"], w=[tfk])
                                S.dve("tensor_tensor", C(out=PBUF[:, c, 0:w - 1], in0=tf[:, 0:w - 1], in1=XNF[:, c, 15:15 + w - 1],
                                                         op=ALU.subtract), r=[tfk, ("A", "XNF", c), ("B", "PBUF", c)], w=[("B", "PBUF", c)])
                    else:
                        pb, pk = bank()
                        for h in range(2):
                            S.pe("matmul", C(out=pb[:, h * 8:(h + 1) * 8], lhsT=SPT[h][0:120, c * 128:(c + 1) * 128],
                                             rhs=SEL[0:120, g, :], start=True, stop=True), r=[("B", "SPT", h), "SEL"], w=[pk])
                        tf, tfk = T[4], ("T", 4)
                        S.dve("tensor_tensor", C(out=tf[:, 0:NS], in0=pb[:, 0:NS], in1=XNS[:, c, :], op=ALU.add),
                              r=[pk, ("A", "XNS", c)], w=[tfk])
                        rel(pk)
                        if g == 0:
                            S.dve("scalar_tensor_tensor", C(out=PBUF[:, c, 0:NS], in0=XNS[:, c, :], scalar=-2.0, in1=tf[:, 0:NS],
                                                            op0=ALU.mult, op1=ALU.add), r=[tfk, ("A", "XNS", c)], w=[("B", "PBUF", c)])
                        else:
                            S.dve("scalar_tensor_tensor", C(out=PBUF[:, c, 0:NS], in0=tf[:, 0:NS], scalar=1.0 / w, in1=XNS[:, c, :],
                                                            op0=ALU.mult, op1=ALU.subtract), r=[tfk, ("A", "XNS", c)], w=[("B", "PBUF", c)])
                if not smp:
                    S.pool("tensor_copy", C(out=XNF[:, :, 0:15], in_=XNF[:, :, N:N + 15]),
                           r=[("A", "XNF", c) for c in range(8)], w=[("A", "XNF", c) for c in range(8)])
                if t == 3:
                    out_T([XNF[:, c, 0:15] for c in range(8)], [("A", "XNF", c) for c in range(8)], 15,
                          lambda g: O["pool_p"][o_, :, g * 512:(g + 1) * 512])
                if t + 1 < 5:
                    do_norm(t + 1)
                if t == 0:
                    fold_scale()
                for m in range(8):
                    g, mo = m // 2, m % 2
                    pb, pk = bank()
                    for ki in range(2):
                        S.pe("matmul", C(out=pb[:, 0:N], lhsT=PW[:, g, ki, mo * 128:(mo + 1) * 128], rhs=PBUF[:, 2 * g + ki, 0:N],
                                         start=(ki == 0), stop=False), r=[("B", "PBUF", 2 * g + ki), ("PW", g)], w=[pk])
                    S.pe("matmul", C(out=pb[:, 0:N], lhsT=IDENT[:], rhs=X[:, m, c0:c0 + N], start=False, stop=True),
                         r=[("X", m, t), "IDENT"], w=[pk])
                    S.act("activation", C(out=X[:, m, c0:c0 + N], in_=pb[:, 0:N], func=AF.Identity,
                                          bias=DV[:, DPBS0 + o_ * 8 + m:DPBS0 + o_ * 8 + m + 1], scale=1.0),
                          r=[pk, ("X", m, t), "DV"], w=[("X", m, t)])
                    rel(pk)
                if t == 4:
                    out_T([XNS[:, c, :] for c in range(8)], [("A", "XNS", c) for c in range(8)], NS,
                          lambda g: O["pool_s"][o_, :, 14, g * 512:(g + 1) * 512])

        def ffn(l, after_tile=None, mid_hook=None, mid_hook2=None):
            fence_all()
            for t in range(5):
                c0, N = TILES[t]
                norm_tile(t, FFN0 + l * 8, lambda c: (XN[:, c, c0:c0 + N], ("A", "XN", c, t)))
            rr = 0
            for q in range(4):
                for h in range(2):
                    u, slot = ws_acquire()
                    W = RING[slot]
                    if q == 0 and h == 0:
                        order = [(mi, t) for t in range(5) for mi in range(4)]
                    else:
                        order = [(mi, t) for mi in range(4) for t in range(5)]
                    for mi, t in order:
                        m = h * 4 + mi
                        c0, N = TILES[t]
                        pb, pk = bank()
                        for k in range(8):
                            S.pe("matmul", C(out=pb[:, 0:N], lhsT=W[:, k, mi * 128:(mi + 1) * 128], rhs=XN[:, k, c0:c0 + N],
                                             start=(k == 0), stop=(k == 7)), r=[("A", "XN", k, t), ("R", slot)], w=[pk])
                        ti = rr % 4
                        rr += 1
                        tt, tk = T[ti], ("T", ti)
                        S.act("activation", C(out=tt[:, 0:N], in_=pb[:, 0:N], func=AF.Relu), r=[pk], w=[tk])
                        rel(pk)
                        eng = S.pool if rr % 3 else S.dve
                        eng("tensor_tensor", C(out=HT[:, m, c0:c0 + N], in0=tt[:, 0:N], in1=tt[:, 0:N], op=ALU.mult),
                            r=[tk], w=[("B", "HT", m, t)])
                    ws_release(u)
                if q == 1 and mid_hook is not None:
                    mid_hook()
                if q == 2 and mid_hook2 is not None:
                    mid_hook2()
                for h in range(2):
                    u, slot = ws_acquire()
                    W = RING[slot]
                    if q == 3 and h == 1:
                        order = [(mi, t) for t in range(5) for mi in range(4)]
                    else:
                        order = [(mi, t) for mi in range(4) for t in range(5)]
                    for mi, t in order:
                        m = h * 4 + mi
                        c0, N = TILES[t]
                        pb, pk = bank()
                        for k in range(8):
                            S.pe("matmul", C(out=pb[:, 0:N], lhsT=W[:, k, mi * 128:(mi + 1) * 128], rhs=HT[:, k, c0:c0 + N],
                                             start=(k == 0), stop=(k == 7)), r=[("B", "HT", k, t), ("R", slot)], w=[pk])
                        S.dve("tensor_tensor", C(out=X[:, m, c0:c0 + N], in0=X[:, m, c0:c0 + N], in1=pb[:, 0:N], op=ALU.add),
                              r=[pk, ("X", m, t)], w=[("X", m, t)])
                        rel(pk)
                        if q == 3 and h == 1 and mi == 3 and after_tile is not None:
                            after_tile(t)
                    ws_release(u)

        def final_tile(t):
            c0, N = TILES[t]
            if t == 0:
                fence_region("A")
                STG_CUR[0] = STG
                STG_KEYS[0] = [("A", "STG", i) for i in range(4)]
            norm_tile(t, FIN0, lambda c: (YF[:, c, 0:N], ("A", "YF", c)))
            if t < 4:
                for tb in range(4):
                    out_T([YF[:, c, tb * 128:(tb + 1) * 128] for c in range(8)], [("A", "YF", c) for c in range(8)], 128,
                          lambda g, tb=tb, t=t: O["y_p"][t * 512 + tb * 128:t * 512 + (tb + 1) * 128, g * 512:(g + 1) * 512])
            else:
                out_T([YF[:, c, 0:NS] for c in range(8)], [("A", "YF", c) for c in range(8)], NS,
                      lambda g: O["y_s"][:, g * 512:(g + 1) * 512])

        load_x_tile(0)
        mixer_begin(0)
        load_x_sample()
        even_setup_consts(0)
        for t in range(1, 4):
            load_x_tile(t)
        even_setup_state(0)
        for l in range(4):
            if l % 2 == 0:
                even_mixer(l)
            else:
                odd_mixer(l)
            if l < 3:
                ffn(l, after_tile=lambda t, l=l: mixer_begin(l + 1) if t == 0 else None,
                    mid_hook=(lambda: (even_setup_consts(2, part="load"), even_setup_state_load(2))) if l == 1 else None,
                    mid_hook2=(lambda: (even_setup_wmt_pe(), even_setup_state_pe(2))) if l == 1 else None)
            else:
                ffn(l, after_tile=final_tile)
        assert len(bank_free) == 8, f"leaked PSUM banks: {bank_free}"
        counts = S.emit()
    return nc, counts


_CACHE = {}


def kernel(**inputs):
    f32 = lambda a: np.ascontiguousarray(np.asarray(a, dtype=np.float32))
    inp = {k: f32(v) for k, v in inputs.items()}
    if "nc" not in _CACHE:
        _CACHE["nc"] = build()[0]
    nc = _CACHE["nc"]
    shared = {k: inp[k] for k in ("norm_mix", "norm_ffn", "norm_final", "w_in", "w_out", "v_norm", "sgu_w", "sgu_b",
                                  "conv_w", "conv_b", "gate_a_w", "gate_a_b", "gate_x_w", "gate_x_b", "lru_lambda",
                                  "pool_w", "pool_b", "pool_scale", "ffn_w1", "ffn_w2")}
    in_maps = []
    for i in range(NCORES):
        sl = slice(i * NS, (i + 1) * NS)
        m = dict(shared)
        m["x_p"] = f32(inp["x_prompt"][i])
        m["x_s"] = f32(inp["x_sample"][sl, 0, :])
        m["st_conv"] = f32(inp["state_conv"][:, sl])
        m["st_h"] = f32(inp["state_rglru"][:, sl])
        m["st_pool"] = f32(inp["state_pool"][:, sl])
        in_maps.append(m)
    res = run_bass_kernel_spmd(nc, in_maps, core_ids=list(range(NCORES)))
    R = res.results
    y_prompt = np.stack([R[i]["y_p"] for i in range(NCORES)], 0)
    y_sample = np.concatenate([R[i]["y_s"] for i in range(NCORES)], 0)[:, None, :]
    sgu_v = np.concatenate([R[i]["sgu_v"] for i in range(NCORES)], 1)[:, :, None, :]
    conv_p = np.stack([R[i]["conv_p"] for i in range(NCORES)], 1)
    conv_s = np.concatenate([R[i]["conv_s"] for i in range(NCORES)], 1)
    h_p = np.stack([R[i]["h_p"] for i in range(NCORES)], 1)
    h_s = np.concatenate([R[i]["h_s"] for i in range(NCORES)], 1)
    pool_p = np.stack([R[i]["pool_p"] for i in range(NCORES)], 1)
    pool_s = np.concatenate([R[i]["pool_s"] for i in range(NCORES)], 1)
    return (y_prompt.astype(np.float32), y_sample.astype(np.float32), sgu_v.astype(np.float32), conv_p.astype(np.float32),
            conv_s.astype(np.float32), h_p.astype(np.float32), h_s.astype(np.float32), pool_p.astype(np.float32),
            pool_s.astype(np.float32))
```

```python
import contextlib
import numpy as np
import concourse.bass as bass
import concourse.mybir as mybir
from concourse.bass_utils import run_bass_kernel_spmd

F32 = mybir.dt.float32
BF16 = mybir.dt.bfloat16
AF = mybir.ActivationFunctionType
ALU = mybir.AluOpType

NCORES = 8
D = 1024
SEQ = 2048
NS = 16
NT = SEQ + NS
TILES = [(0, 512), (512, 512), (1024, 512), (1536, 512), (2048, 16)]
EPS = 1e-6
COMPUTE = ("pe", "act", "dve", "pool")
EPOCH = 24000
REGIONS = ("A", "B")


def C(*args, **kwargs):
    return (args, kwargs)


class Op:
    __slots__ = ("eng", "fn", "r", "w", "dma", "deps", "signal", "tok", "idx", "name")

    def __init__(self, eng, fn, r, w, dma, name):
        self.eng, self.fn, self.r, self.w, self.dma, self.name = eng, fn, tuple(r), tuple(w), dma, name
        self.deps = set()
        self.signal = False
        self.tok = None


class Sched:
    def __init__(self, nc, n_dma_sems=8):
        self.nc = nc
        self.ops = []
        self.last_w = {}
        self.readers = {}
        self.n_dma_sems = n_dma_sems
        self.region_ops = {r: [] for r in REGIONS}
        self.fence_of = {r: None for r in REGIONS}

    def op(self, eng, meth, call, r=(), w=(), dma=False, name=""):
        o = Op(eng, (meth, call), r, w, dma, name or meth)
        o.idx = len(self.ops)
        for k in o.r:
            lw = self.last_w.get(k)
            if lw is not None:
                o.deps.add(lw)
        for k in o.w:
            lw = self.last_w.get(k)
            if lw is not None:
                o.deps.add(lw)
            for rd in self.readers.get(k, ()):
                o.deps.add(rd)
        for k in o.r:
            self.readers.setdefault(k, []).append(o)
        for k in o.w:
            self.last_w[k] = o
            self.readers[k] = []
        regs = set()
        for k in o.r + o.w:
            if isinstance(k, tuple) and k[0] in self.region_ops:
                regs.add(k[0])
        for rg in regs:
            self.region_ops[rg].append(o)
            if self.fence_of[rg] is not None:
                o.deps.add(self.fence_of[rg])
        o.deps.discard(o)
        self.ops.append(o)
        return o

    def fence(self, region, meth, call):
        o = Op("pool", (meth, call), (), (), False, "fence_" + region)
        o.idx = len(self.ops)
        for p in self.region_ops[region]:
            o.deps.add(p)
        if self.fence_of[region] is not None:
            o.deps.add(self.fence_of[region])
        self.region_ops[region] = []
        self.fence_of[region] = o
        self.ops.append(o)
        return o

    def pe(self, meth, call, r=(), w=(), name=""):
        return self.op("pe", meth, call, r, w, name=name)

    def act(self, meth, call, r=(), w=(), name=""):
        return self.op("act", meth, call, r, w, name=name)

    def dve(self, meth, call, r=(), w=(), name=""):
        return self.op("dve", meth, call, r, w, name=name)

    def pool(self, meth, call, r=(), w=(), name=""):
        return self.op("pool", meth, call, r, w, name=name)

    def dma(self, meth, call, r=(), w=(), q="sp", name=""):
        return self.op(q, meth, call, r, w, dma=True, name=name)

    def _needs_sync(self, d, o):
        if d.dma:
            return True
        if d.eng != o.eng:
            return True
        if o.dma:
            return True
        if d.eng == "pe":
            return False
        if d.name.startswith("fence_"):
            return True
        return bool(set(d.w) & (set(o.r) | set(o.w))) or bool(set(d.r) & set(o.w))

    def emit(self, final_wait_eng="sp"):
        nc = self.nc
        ops = self.ops
        for o in ops:
            for d in o.deps:
                if self._needs_sync(d, o):
                    d.signal = True
            if o.dma:
                o.signal = True
        stack = contextlib.ExitStack()
        eng_sems, eng_cnt, eng_epoch = {}, {}, {}

        def new_sem(name):
            return stack.enter_context(nc.semaphore(name))

        for e in COMPUTE:
            eng_epoch[e] = 0
            eng_cnt[e] = 0
            eng_sems[e] = new_sem(f"c_{e}_0")
        dma_sems = [new_sem(f"d_{i}") for i in range(self.n_dma_sems)]
        dma_cnt = [0] * self.n_dma_sems
        dma_last = [None] * self.n_dma_sems
        dma_rr = 0
        for o in ops:
            if not o.signal:
                continue
            if o.dma and o.eng == "pool":
                o.tok = (new_sem(f"sw_{o.idx}"), 16)
            elif o.dma:
                i = dma_rr % self.n_dma_sems
                dma_rr += 1
                if dma_last[i] is not None:
                    o.deps.add(dma_last[i])
                dma_cnt[i] += 16
                o.tok = (dma_sems[i], dma_cnt[i])
                dma_last[i] = o
            else:
                e = o.eng
                if eng_cnt[e] >= EPOCH:
                    eng_epoch[e] += 1
                    eng_cnt[e] = 0
                    eng_sems[e] = new_sem(f"c_{e}_{eng_epoch[e]}")
                eng_cnt[e] += 1
                o.tok = (eng_sems[e], eng_cnt[e])
        streams = {}
        for o in ops:
            streams.setdefault(o.eng, []).append(o)
        all_dma = [o for o in ops if o.dma]

        def run_stream(e, eng_obj):
            seen = {}
            for o in streams.get(e, []):
                waits = {}
                for d in o.deps:
                    if not self._needs_sync(d, o):
                        continue
                    sem, val = d.tok
                    key = id(sem)
                    if seen.get(key, 0) >= val:
                        continue
                    if key not in waits or waits[key][1] < val:
                        waits[key] = (sem, val)
                for key, (sem, val) in waits.items():
                    eng_obj.wait_ge(sem, val)
                    seen[key] = val
                meth, (cargs, ckw) = o.fn
                ins = getattr(eng_obj, meth)(*cargs, **ckw)
                if o.signal:
                    assert ins is not None, f"op {o.name} returned no instruction"
                    ins.then_inc(o.tok[0], 16 if o.dma else 1)
            if e == final_wait_eng:
                fin = {}
                for d in all_dma:
                    sem, val = d.tok
                    if fin.get(id(sem), (None, 0))[1] < val:
                        fin[id(sem)] = (sem, val)
                for key, (sem, val) in fin.items():
                    if seen.get(key, 0) < val:
                        eng_obj.wait_ge(sem, val)
                for ce in COMPUTE:
                    last = None
                    for o in streams.get(ce, []):
                        if o.signal:
                            last = o
                    if last is not None:
                        sem, val = last.tok
                        if seen.get(id(sem), 0) < val:
                            eng_obj.wait_ge(sem, val)

        with stack:
            with nc.Block() as block:
                @block.sync
                def _(eng):
                    run_stream("sp", eng)

                @block.tensor
                def _(eng):
                    run_stream("pe", eng)

                @block.scalar
                def _(eng):
                    run_stream("act", eng)

                @block.vector
                def _(eng):
                    run_stream("dve", eng)

                @block.gpsimd
                def _(eng):
                    run_stream("pool", eng)
        return {e: len(v) for e, v in streams.items()}


IN_SHAPES = {
    "x_p": [SEQ, D], "x_s": [NS, D], "st_conv": [2, NS, 3, 512], "st_h": [2, NS, 512],
    "st_pool": [2, NS, 15, D], "norm_mix": [4, D], "norm_ffn": [4, D], "norm_final": [D],
    "w_in": [2, D, 2048], "w_out": [2, D, D], "v_norm": [2, 512], "sgu_w": [2, 8, 128, 128],
    "sgu_b": [2, 8, 128], "conv_w": [2, 4, 512], "conv_b": [2, 512], "gate_a_w": [2, 8, 64, 64],
    "gate_a_b": [2, 512], "gate_x_w": [2, 8, 64, 64], "gate_x_b": [2, 512], "lru_lambda": [2, 512],
    "pool_w": [2, 4, 256, 256], "pool_b": [2, D], "pool_scale": [2, D],
    "ffn_w1": [4, D, 4096], "ffn_w2": [4, 4096, D],
}
OUT_SHAPES = {
    "y_p": [SEQ, D], "y_s": [NS, D], "sgu_v": [2, NS, 512], "conv_p": [2, 3, 512],
    "conv_s": [2, NS, 3, 512], "h_p": [2, 512], "h_s": [2, NS, 512],
    "pool_p": [2, 15, D], "pool_s": [2, NS, 15, D],
}

MIX0, FFN0, FIN0, CW0, CB0, GAB0, GXB0, LAM0, PB0, PS0 = 0, 32, 64, 72, 104, 112, 120, 128, 136, 152
DC0, DHC0, DHBA0, DHBX0, DPBS0 = 0, 8, 16, 24, 32


def build():
    nc = bass.Bass("TRN2", target_bir_lowering=False)
    I = {k: nc.dram_tensor(k, s, F32, kind="ExternalInput").ap() for k, s in IN_SHAPES.items()}
    O = {k: nc.dram_tensor(k, s, F32, kind="ExternalOutput").ap() for k, s in OUT_SHAPES.items()}
    st = contextlib.ExitStack()
    with st:
        def sb(name, shape, dt):
            return st.enter_context(nc.sbuf_tensor(name, shape, dt))

        X = sb("X", [128, 8, NT], F32)
        A = sb("A", [128, 16512], BF16)
        B = sb("B", [128, 16512], BF16)
        RING = [sb(f"ring{i}", [128, 8, 512], BF16) for i in range(4)]
        SQ = sb("SQ", [128, 8, 512], BF16)
        T = [sb(f"T{i}", [128, 512], F32) for i in range(6)]
        IDENT = sb("IDENT", [128, 128], F32)
        ONES = sb("ONES", [128, 128], BF16)
        VECT = sb("VECT", [128, 168], F32)
        DV = sb("DV", [128, 48], F32)
        CONSTS = sb("CONSTS", [128, 4], F32)
        INVCNT = sb("INVCNT", [128, 16], F32)
        WCNT = sb("WCNT", [128, 2, 4], F32)
        SEL = sb("SEL", [128, 4, 8], F32)
        WMT = sb("WMT", [128, 8, 128], BF16)
        WR = sb("WR", [128, 1024], F32)
        WRAW = WR[:, :].rearrange("p (h s) -> p h s", h=8)
        NTS = [sb(f"NT{i}", [128, 512], F32) for i in range(3)]
        XSS_ = WR
        BH = sb("BH", [128, 4, 128], F32)
        W0H = sb("W0H", [128, 4], F32)
        VNB = sb("VNB", [128, 512], F32)
        WBD = sb("WBD", [128, 2, 4, 128], BF16)
        PW = sb("PW", [128, 4, 2, 256], BF16)
        HCAR = sb("HCAR", [128, 4], F32)
        XBS = sb("XBS", [128, 4, 16], F32)
        HSS = sb("HSS", [128, 4, 16], F32)
        HIST = sb("HIST", [128, 12, 16], F32)
        H0 = sb("H0", [128, 4, 16], F32)
        SMALL = sb("SMALL", [128, 8], F32)
        VFM = sb("VFM", [128, 64], F32)
        DUMMY = sb("DUMMY", [128, 64], F32)
        PS = [st.enter_context(nc.psum_tensor(f"ps{i}", [128, 512], F32)) for i in range(8)]

        S = Sched(nc)
        bank_rr = [0]
        fence_n = [0]

        bank_free = list(range(8))

        def bank():
            assert bank_free, "no free PSUM bank"
            i = bank_free.pop(0)
            return PS[i], ("P", i)

        def rel(pk):
            assert pk[1] not in bank_free
            bank_free.append(pk[1])

        def fence_region(rg):
            i = fence_n[0] % 64
            fence_n[0] += 1
            S.fence(rg, "memset", C(DUMMY[0:1, i:i + 1], 0.0))

        def fence_all():
            for rg in REGIONS:
                fence_region(rg)

        def even_norm(l, tn):
            nn = TILES[tn][1]
            norm_tile(tn, MIX0 + l * 8, lambda c: (XNT[:, c, 0:nn], ("A", "XNT", c)))

        def odd_norm(l, tn):
            nn = TILES[tn][1]
            if tn < 4:
                norm_tile(tn, MIX0 + l * 8, lambda c: (XNF[:, c, 15:15 + nn], ("A", "XNF", c)))
            else:
                norm_tile(tn, MIX0 + l * 8, lambda c: (XNS[:, c, :], ("A", "XNS", c)))

        def mixer_begin(l):
            fence_region("A")
            if l % 2 == 0:
                even_norm(l, 0)
            else:
                S.pool("memset", C(XNF[:, :, 0:15], 0.0), w=[("A", "XNF", c) for c in range(8)])
                odd_norm(l, 0)

        def a_bf(off, n):
            return A[:, off:off + n]

        def a_f32(off, n):
            return A[:, off:off + 2 * n].bitcast(F32)

        def b_bf(off, n):
            return B[:, off:off + n]

        def b_f32(off, n):
            return B[:, off:off + 2 * n].bitcast(F32)

        XN = A[:, 0:8 * NT].rearrange("p (c n) -> p c n", c=8)
        HT = B[:, 0:8 * NT].rearrange("p (c n) -> p c n", c=8)
        XNT = a_bf(0, 4096).rearrange("p (c n) -> p c n", c=8)
        VTM = a_bf(4096, 2048).rearrange("p (c n) -> p c n", c=4)
        CAT = a_bf(6144, 4096).rearrange("p (c n) -> p c n", c=8)
        G2 = [a_f32(10240, 512), a_f32(11264, 512)]
        AT = [a_f32(12288 + 1024 * i, 512) for i in range(4)]
        WO = [b_bf(0, 4096).rearrange("p (c n) -> p c n", c=8), b_bf(4096, 4096).rearrange("p (c n) -> p c n", c=8)]
        XBUF = b_f32(8192, 4 * 516).rearrange("p (c n) -> p c n", c=4)
        BT = [b_f32(8192 + 2 * 4 * 516 + 1024 * i, 512) for i in range(3)]
        XNF = a_f32(0, 8 * 528).rearrange("p (c n) -> p c n", c=8)
        SW = [a_f32(2 * 8 * 528 + 2 * 528 * i, 528) for i in range(4)]
        XNS = a_f32(2 * 8 * 528 + 8 * 528, 128).rearrange("p (c n) -> p c n", c=8)
        PBUF = b_bf(0, 4096).rearrange("p (c n) -> p c n", c=8)
        SPT = [b_f32(4096, 1024), b_f32(4096 + 2048, 1024)]
        PWF = b_f32(8192, 2048).rearrange("p (g k n) -> p g k n", g=4, k=2)
        PSB = b_f32(12288, 1024)
        XS = [b_f32(0, 4096).rearrange("p (c n) -> p c n", c=4), b_f32(8192, 4096).rearrange("p (c n) -> p c n", c=4)]
        XSS = None
        YF = a_f32(0, 4096).rearrange("p (c n) -> p c n", c=8)
        STG = [a_f32(8192, 512), a_f32(9216, 512), a_f32(10240, 512), a_f32(11264, 512)]

        EPS1 = CONSTS[:, 0:1]
        EPS4 = CONSTS[:, 1:2]

        S.pool("memset", C(IDENT[:], 1.0), w=["IDENT"])
        S.pool("affine_select", C(out=IDENT[:], in_=IDENT[:], pattern=[[-1, 128]], compare_op=ALU.is_equal,
                                         fill=0.0, base=0, channel_multiplier=1), r=["IDENT"], w=["IDENT"])
        S.pool("memset", C(ONES[:], 1.0), w=["ONES"])
        S.pool("memset", C(CONSTS[:, 0:1], EPS), w=["CONSTS"])
        S.pool("memset", C(CONSTS[:, 1:2], 4 * EPS), r=["CONSTS"], w=["CONSTS"])
        S.pool("memset", C(CONSTS[:, 2:3], -0.5), r=["CONSTS"], w=["CONSTS"])
        S.pool("memset", C(CONSTS[:, 3:4], 1e-18), r=["CONSTS"], w=["CONSTS"])
        for j in range(16):
            S.pool("memset", C(INVCNT[:, j:j + 1], 1.0 / (j + 1)), r=["INVCNT"] if j else [], w=["INVCNT"])
        for g in range(2):
            w = 2 << g
            for j in range(w - 1):
                S.pool("memset", C(WCNT[:, g, j:j + 1], float(w) / (j + 1)), r=["WCNT"] if (g or j) else [], w=["WCNT"])
        S.pool("memset", C(SEL[:], 1.0), w=["SEL"])
        for g in range(4):
            w = 2 << g
            S.pool("affine_select", C(out=SEL[0:120, g, :], in_=SEL[0:120, g, :], pattern=[[-15, 8]],
                                                       compare_op=ALU.is_ge, fill=0.0, base=-(16 - w),
                                                       channel_multiplier=1), r=["SEL"], w=["SEL"])
            S.pool("affine_select", C(out=SEL[0:120, g, :], in_=SEL[0:120, g, :], pattern=[[15, 8]],
                                                  compare_op=ALU.is_ge, fill=0.0, base=14,
                                                  channel_multiplier=-1), r=["SEL"], w=["SEL"])
        for t in range(2):
            S.dma("dma_start", C(out=XS[t], in_=I["x_p"][t * 512:(t + 1) * 512, :].rearrange("(tb p) d -> p tb d", p=128)),
                  w=[("B", "XS", t)])
        VRA, VRB = T[0], T[1]
        srcs_a = [
            (I["norm_mix"].rearrange("l (c p) -> (l c) p", p=128), 0, 32),
            (I["norm_ffn"].rearrange("l (c p) -> (l c) p", p=128), 32, 32),
            (I["norm_final"].rearrange("(c p) -> c p", p=128), 64, 8),
            (I["conv_w"].rearrange("e k (c p) -> (e k c) p", p=128), 72, 32),
            (I["conv_b"].rearrange("e (c p) -> (e c) p", p=128), 104, 8),
            (I["gate_a_b"].rearrange("e (c p) -> (e c) p", p=128), 112, 8),
            (I["gate_x_b"].rearrange("e (c p) -> (e c) p", p=128), 120, 8),
        ]
        for src, r0, n in srcs_a:
            S.dma("dma_start", C(out=VRA[r0:r0 + n, 0:128], in_=src), w=[("T", 0)])
        srcs_b = [
            (I["lru_lambda"].rearrange("e (c p) -> (e c) p", p=128), 0, 8),
            (I["pool_b"].rearrange("o (c p) -> (o c) p", p=128), 8, 16),
            (I["pool_scale"].rearrange("o (c p) -> (o c) p", p=128), 24, 16),
        ]
        for src, r0, n in srcs_b:
            S.dma("dma_start", C(out=VRB[r0:r0 + n, 0:128], in_=src), w=[("T", 1)])
        pb, pk = bank()
        S.pe("transpose", C(out=pb[:, 0:128], in_=VRA[:, 0:128], identity=IDENT[:]), r=[("T", 0), "IDENT"], w=[pk])
        S.pe("transpose", C(out=pb[:, 128:168], in_=VRB[0:40, 0:128], identity=IDENT[0:40, 0:40]),
             r=[("T", 1), "IDENT"], w=[pk])
        S.dve("tensor_copy", C(out=VECT[:], in_=pb[:, 0:168]), r=[pk], w=["VECT"])
        rel(pk)
        S.act("activation", C(out=SMALL[:, 0:8], in_=VECT[:, LAM0:LAM0 + 8], func=AF.Exp, scale=-1.0),
              r=["VECT"], w=["SMALL"])
        S.act("activation", C(out=SMALL[:, 0:8], in_=SMALL[:, 0:8], func=AF.Ln, bias=1.0), r=["SMALL"], w=["SMALL"])
        S.dve("tensor_scalar", C(out=DV[:, DC0:DC0 + 8], in0=SMALL[:, 0:8], scalar1=-8.0, scalar2=None,
                                        op0=ALU.mult), r=["SMALL"], w=["DV"])
        S.dve("tensor_scalar", C(out=DV[:, DHC0:DHC0 + 8], in0=SMALL[:, 0:8], scalar1=-4.0, scalar2=None,
                                        op0=ALU.mult), r=["SMALL", "DV"], w=["DV"])
        S.dve("tensor_scalar", C(out=DV[:, DHBA0:DHBA0 + 16], in0=VECT[:, GAB0:GAB0 + 16], scalar1=0.5, scalar2=None,
                                        op0=ALU.mult), r=["VECT", "DV"], w=["DV"])
        S.dve("tensor_tensor", C(out=DV[:, DPBS0:DPBS0 + 16], in0=VECT[:, PB0:PB0 + 16], in1=VECT[:, PS0:PS0 + 16],
                                        op=ALU.mult), r=["VECT", "DV"], w=["DV"])
        units = []
        for l in range(4):
            if l % 2 == 0:
                wi = I["w_in"][l // 2].rearrange("(k p) n -> p k n", p=128)
                for blk in (1, 2, 3, 0):
                    units.append(wi[:, :, blk * 512:(blk + 1) * 512])
            w1 = I["ffn_w1"][l].rearrange("(k p) n -> p k n", p=128)
            w2 = I["ffn_w2"][l].rearrange("(k p) n -> p k n", p=128)
            for q in range(4):
                for h in range(2):
                    units.append(w1[:, :, q * 1024 + h * 512:q * 1024 + (h + 1) * 512])
                for h in range(2):
                    units.append(w2[:, q * 8:(q + 1) * 8, h * 512:(h + 1) * 512])
        ws = {"free": [0, 1, 2, 3], "next": 0, "loaded": {}, "cur": 0}

        def ws_issue():
            while ws["free"] and ws["next"] < len(units):
                slot = ws["free"].pop(0)
                u = ws["next"]
                ws["next"] += 1
                S.dma("dma_start", C(out=RING[slot][:], in_=units[u]), w=[("R", slot)], q="pool")
                ws["loaded"][u] = slot

        def ws_acquire():
            u = ws["cur"]
            ws["cur"] += 1
            assert u in ws["loaded"], "weight unit not issued"
            return u, ws["loaded"][u]

        def ws_release(u):
            slot = ws["loaded"].pop(u)
            ws["free"].append(slot)
            ws_issue()

        ws_issue()

        cp_rr = [0]

        def evac_copy(out_ap, in_ap, r, w):
            if cp_rr[0] % 2 == 0:
                S.act("activation", C(out=out_ap, in_=in_ap, func=AF.Copy), r=r, w=w)
            else:
                S.dve("tensor_copy", C(out=out_ap, in_=in_ap), r=r, w=w)
            cp_rr[0] += 1

        def load_x_tile(t):
            xs = XS[t % 2]
            xk = ("B", "XS", t % 2)
            if t >= 2:
                S.dma("dma_start", C(out=xs, in_=I["x_p"][t * 512:(t + 1) * 512, :].rearrange(
                    "(tb p) d -> p tb d", p=128)), w=[xk])
            for c in range(8):
                pb, pk = bank()
                for tb in range(4):
                    S.pe("transpose", C(out=pb[:, tb * 128:(tb + 1) * 128], in_=xs[:, tb, c * 128:(c + 1) * 128],
                                        identity=IDENT[:]), r=[xk, "IDENT"], w=[pk])
                evac_copy(X[:, c, t * 512:(t + 1) * 512], pb[:, :], [pk], [("X", c, t)])
                rel(pk)

        def load_x_sample():
            S.dma("dma_start", C(out=XSS_[0:NS, :], in_=I["x_s"]), w=[("WR", 0), ("WR", 1)])
            pb, pk = bank()
            for c in range(8):
                S.pe("transpose", C(out=pb[:, c * 16:(c + 1) * 16], in_=XSS_[0:NS, c * 128:(c + 1) * 128],
                                    identity=IDENT[0:NS, 0:NS]), r=[("WR", 0), ("WR", 1), "IDENT"], w=[pk])
            S.dve("tensor_copy", C(out=X[:, :, SEQ:NT], in_=pb[:, 0:128].rearrange("p (c n) -> p c n", c=8)),
                  r=[pk], w=[("X", c, 4) for c in range(8)])
            rel(pk)
            for e_ in range(2):
                S.dma("dma_start", C(out=O["conv_s"][e_, :, 0:2, :], in_=I["st_conv"][e_, :, 1:3, :]))
                S.dma("dma_start", C(out=O["pool_s"][e_, :, 0:14, :], in_=I["st_pool"][e_, :, 1:15, :]))

        def norm_tile(t, gcol, out_fn, split_sq=False):
            c0, N = TILES[t]
            pb, pk = bank()
            for c in range(8):
                s_ = c % 6
                S.act("activation", C(out=SQ[:, s_, 0:N], in_=X[:, c, c0:c0 + N], func=AF.Square),
                      r=[("X", c, t)], w=[("SQ", s_)])
                S.pe("matmul", C(out=pb[:, 0:N], lhsT=ONES[:], rhs=SQ[:, s_, 0:N], start=(c == 0), stop=(c == 7)),
                     r=[("SQ", s_), "ONES"], w=[pk])
            rs = T[5]
            S.act("activation", C(out=rs[:, 0:N], in_=pb[:, 0:N], func=AF.Ln, bias=EPS1, scale=1.0 / D),
                  r=[pk, "CONSTS"], w=[("T", 5)])
            rel(pk)
            S.act("activation", C(out=rs[:, 0:N], in_=rs[:, 0:N], func=AF.Exp, scale=-0.5), r=[("T", 5)], w=[("T", 5)])
            for c in range(8):
                oap, okey = out_fn(c)
                S.dve("scalar_tensor_tensor", C(out=oap, in0=X[:, c, c0:c0 + N], scalar=VECT[:, gcol + c:gcol + c + 1],
                                                in1=rs[:, 0:N], op0=ALU.mult, op1=ALU.mult),
                      r=[("X", c, t), ("T", 5), "VECT"], w=[okey])

        def gelu2(pbap, pk, out_ap, okey, tmp, tk, P, N):
            tt = tmp[0:P, 0:N]
            S.act("activation", C(out=tt, in_=pbap, func=AF.Square, scale=0.044715 ** 0.5), r=[pk], w=[tk])
            yield
            S.dve("scalar_tensor_tensor", C(out=tt, in0=tt, scalar=1.0, in1=pbap, op0=ALU.add, op1=ALU.mult), r=[tk, pk], w=[tk])
            yield
            S.act("activation", C(out=tt, in_=tt, func=AF.Tanh, scale=0.7978845608028654), r=[tk], w=[tk])
            yield
            S.dve("scalar_tensor_tensor", C(out=out_ap, in0=tt, scalar=1.0, in1=pbap, op0=ALU.add, op1=ALU.mult),
                  r=[tk, pk], w=[okey])
            yield

        def run_multi(groups):
            state = [{"it": iter(ch), "k": k, "active": []} for ch, k in groups]
            while True:
                progressed = False
                for st_ in state:
                    while len(st_["active"]) < st_["k"]:
                        nxt = next(st_["it"], None)
                        if nxt is None:
                            break
                        st_["active"].append(nxt)
                    for g in list(st_["active"]):
                        try:
                            next(g)
                            progressed = True
                        except StopIteration:
                            st_["active"].remove(g)
                            progressed = True
                if not progressed:
                    break

        stg_rr = [0]

        def out_T(srcs, skeys, n, dst_fn):
            for g in range(len(srcs) // 4):
                pb, pk = bank()
                for j in range(4):
                    src = srcs[g * 4 + j]
                    S.pe("transpose", C(out=pb[0:n, j * 128:(j + 1) * 128], in_=src,
                                                                  identity=IDENT[:]), r=[skeys[g * 4 + j], "IDENT"], w=[pk])
                si = stg_rr[0] % 4
                stg_rr[0] += 1
                stg, sk = STG_CUR[0][si], STG_KEYS[0][si]
                evac_copy(stg[0:n, :], pb[0:n, :], [pk], [sk])
                rel(pk)
                S.dma("dma_start", C(out=dst_fn(g), in_=stg[0:n, :]), r=[sk])

        STG_CUR = [[T[0], T[1], T[2], T[3]]]
        STG_KEYS = [[("T", 0), ("T", 1), ("T", 2), ("T", 3)]]

        wbd_keys = [("WBDl", mi, hh) for mi in range(2) for hh in range(2)]

        def even_setup_wmt_pe():
            for h in range(8):
                pb, pk = bank()
                S.pe("transpose", C(out=pb[:, 0:128], in_=WRAW[:, h, :], identity=IDENT[:]),
                     r=[("WR", 0), ("WR", 1), "IDENT"], w=[pk])
                evac_copy(WMT[:, h, :], pb[:, 0:128], [pk], [("WMT", h)])
                rel(pk)

        def even_setup_consts(l, part="all"):
            e_ = l // 2
            S.dma("dma_start", C(out=WRAW[:], in_=I["sgu_w"][e_].rearrange("h t s -> t h s")), w=[("WR", 0), ("WR", 1)])
            S.pool("affine_select", C(out=WRAW[:], in_=WRAW[:], pattern=[[0, 8], [-1, 128]], compare_op=ALU.is_ge,
                                             fill=0.0, base=0, channel_multiplier=1), r=[("WR", 0), ("WR", 1)], w=[("WR", 0), ("WR", 1)])
            if part == "all":
                even_setup_wmt_pe()
            gi = fence_n[0] % 64
            fence_n[0] += 1
            S.pool("memset", C(DUMMY[0:1, gi:gi + 1], 0.0), w=["BH", "W0H"] + [("BHp", h) for h in range(8)] + [("W0p", h) for h in range(8)])
            for h in range(8):
                S.dma("dma_start", C(out=BH[(h % 2) * 64:(h % 2) * 64 + 64, h // 2, :],
                                     in_=I["sgu_b"][e_, h, :].partition_broadcast(64)), r=["BH"], w=[("BHp", h)], q="act")
                S.dma("dma_start", C(out=W0H[(h % 2) * 64:(h % 2) * 64 + 64, h // 2:h // 2 + 1],
                                     in_=I["sgu_w"][e_, h, 0:1, 0:1].rearrange("a b -> (a b)").partition_broadcast(64)),
                      r=["W0H"], w=[("W0p", h)], q="act")
            S.pool("tensor_scalar", C(out=BH[:], in0=BH[:], scalar1=0.5, scalar2=None, op0=ALU.mult),
                   r=[("BHp", h) for h in range(8)], w=["BH"] + [("BHp", h) for h in range(8)])
            S.pool("tensor_scalar", C(out=W0H[:], in0=W0H[:], scalar1=0.5, scalar2=None, op0=ALU.mult),
                   r=[("W0p", h) for h in range(8)], w=["W0H"] + [("W0p", h) for h in range(8)])
            S.dma("dma_start", C(out=VNB[:], in_=I["v_norm"][e_].partition_broadcast(128)), w=["VNB"], q="act")
            S.pool("memset", C(WBD[:], 0.0), w=["WBD"])
            for mi, nm in enumerate(("gate_a_w", "gate_x_w")):
                for hh in range(2):
                    S.dma("dma_start", C(
                        out=WBD[hh * 64:(hh + 1) * 64, mi, :, hh * 64:(hh + 1) * 64],
                        in_=I[nm][e_, hh::2].rearrange("j i o -> i j o")), r=["WBD"], w=[("WBDl", mi, hh)], q="pool")

        def even_setup_state(l):
            e_ = l // 2
            scs = WR[:, 512:1024]
            S.dma("dma_start", C(out=scs[0:NS, :], in_=I["st_h"][e_]),
                  w=[("WR", 1)])
            pb, pk = bank()
            for j in range(4):
                S.pe("transpose", C(out=pb[:, j * 16:(j + 1) * 16], in_=scs[0:NS, j * 128:(j + 1) * 128],
                                                       identity=IDENT[0:NS, 0:NS]), r=[("WR", 1), "IDENT"], w=[pk])
            S.dve("tensor_scalar", C(out=H0[:], in0=pb[:, 0:64].rearrange("p (c n) -> p c n", c=4), scalar1=0.5, scalar2=None, op0=ALU.mult), r=[pk], w=["H0"])
            rel(pk)
            for k in range(3):
                S.dma("dma_start", C(out=scs[0:NS, :], in_=I["st_conv"][e_, :, k, :]), w=[("WR", 1)])
                pb, pk = bank()
                for j in range(4):
                    S.pe("transpose", C(out=pb[:, j * 16:(j + 1) * 16], in_=scs[0:NS, j * 128:(j + 1) * 128],
                                                           identity=IDENT[0:NS, 0:NS]), r=[("WR", 1), "IDENT"], w=[pk])
                S.dve("tensor_copy", C(out=HIST[:, k * 4:(k + 1) * 4, :],
                                                          in_=pb[:, 0:64].rearrange("p (c n) -> p c n", c=4)), r=[pk], w=[("HIST", k)])
                rel(pk)

        STATE_SCR = [(T[4], ("T", 4)), (NTS[0], ("NT", 0)), (NTS[1], ("NT", 1)), (NTS[2], ("NT", 2))]

        def even_setup_state_load(l):
            e_ = l // 2
            srcs = [I["st_h"][e_]] + [I["st_conv"][e_, :, k, :] for k in range(3)]
            for (scr, sk), src_ap in zip(STATE_SCR, srcs):
                S.dma("dma_start", C(out=scr[0:NS, :], in_=src_ap), w=[sk])

        def even_setup_state_pe(l):
            for i, (scr, sk) in enumerate(STATE_SCR):
                pb, pk = bank()
                for j in range(4):
                    S.pe("transpose", C(out=pb[:, j * 16:(j + 1) * 16], in_=scr[0:NS, j * 128:(j + 1) * 128],
                                        identity=IDENT[0:NS, 0:NS]), r=[sk, "IDENT"], w=[pk])
                if i == 0:
                    S.dve("tensor_scalar", C(out=H0[:], in0=pb[:, 0:64].rearrange("p (c n) -> p c n", c=4), scalar1=0.5,
                                             scalar2=None, op0=ALU.mult), r=[pk], w=["H0"])
                else:
                    k = i - 1
                    S.dve("tensor_copy", C(out=HIST[:, k * 4:(k + 1) * 4, :], in_=pb[:, 0:64].rearrange("p (c n) -> p c n", c=4)),
                          r=[pk], w=[("HIST", k)])
                rel(pk)

        def even_mixer(l):
            e_ = l // 2
            fence_region("B")
            wo = I["w_out"][e_].rearrange("(k p) n -> p k n", p=128)
            for h in range(2):
                S.dma("dma_start", C(out=WO[h], in_=wo[:, :, h * 512:(h + 1) * 512]), w=[("B", "WO", h)], q="pool")
            S.pool("memset", C(XBUF[:, :, 0:3], 0.0), w=[("B", "XBUF", j) for j in range(4)])
            S.pool("memset", C(HCAR[:], 0.0), w=[("HCAR", j) for j in range(4)])
            uv, sv = ws_acquire()
            ug, sg = ws_acquire()
            ux, sx = ws_acquire()
            uu, su = ws_acquire()
            Wv, Wu, Wg, Wx = RING[sv], RING[su], RING[sg], RING[sx]
            Kv, Ku, Kg, Kx = ("R", sv), ("R", su), ("R", sg), ("R", sx)
            cvec = lambda base, j: VECT[:, base + e_ * 4 + j:base + e_ * 4 + j + 1]
            dvec = lambda base, j: DV[:, base + e_ * 4 + j:base + e_ * 4 + j + 1]

            WKEYS = [("WR", 0), ("WR", 1)]
            GEL_VU = [(T[0], ("T", 0)), (T[1], ("T", 1))]
            G2U = [(G2[0], ("A", "G2", 0)), (G2[1], ("A", "G2", 1))]
            TSP = [(AT[0], ("A", "AT", 0)), (AT[1], ("A", "AT", 1))]
            GEL_L = [(T[2], ("T", 2)), (T[3], ("T", 3))]
            GG2 = [(AT[2], ("A", "AT", 2)), (AT[3], ("A", "AT", 3))]
            XCP = [(BT[0], ("B", "BT", 0)), (BT[1], ("B", "BT", 1))]
            THR = [(T[4], ("T", 4)), (BT[2], ("B", "BT", 2))]
            THI = [(NTS[0], ("NT", 0)), (NTS[1], ("NT", 1))]
            A2P = [(NTS[2], ("NT", 2)), (WR[:, 0:512], ("WR", 0))]
            XCBP = [(SQ[:, 6, :], ("SQ", 6)), (SQ[:, 7, :], ("SQ", 7))]

            NTL = 5
            n_xnt = [16, 16, 16, 16, 13]
            xnt_reads = [0] * NTL
            norm_done = [True] + [False] * (NTL - 1)
            vtm_cnt = [0] * NTL
            sgu_done = [0] * NTL
            v_done = [False] * NTL
            cat_cnt = [0] * NTL
            w_mm = [0] * NTL
            g_done = [[False] * 4 for _ in range(NTL)]
            x_done = [[False] * 4 for _ in range(NTL)]
            xnt_keys = [("A", "XNT", c) for c in range(8)]

            def n_chain(t):
                while xnt_reads[t - 1] < n_xnt[t - 1]:
                    yield
                even_norm(l, t)
                norm_done[t] = True
                yield

            def v_chain(t, tb, p):
                c0, N = TILES[t]
                smp = (t == 4)
                P = NS if smp else 128
                while not norm_done[t]:
                    yield
                while not bank_free:
                    yield
                pb, pk = bank()
                for k in range(8):
                    S.pe("matmul", C(out=pb[0:P, :], lhsT=XNT[:, k, tb * 128:tb * 128 + P], rhs=Wv[:, k, :],
                                     start=(k == 0), stop=(k == 7)), r=[xnt_keys[k], Kv], w=[pk])
                xnt_reads[t] += 1
                yield
                g2, g2k = G2U[p]
                gel, gk = GEL_VU[p]
                yield from gelu2(pb[0:P, :], pk, g2[0:P, :], g2k, gel, gk, P, 512)
                rel(pk)
                sm, smk = SMALL[0:P, p:p + 1], ("SMALL", p)
                S.act("activation", C(out=gel[0:P, :], in_=g2[0:P, :], func=AF.Square, scale=512.0 ** -0.5, accum_out=sm),
                      r=[g2k], w=[gk, smk])
                yield
                S.pool("tensor_scalar", C(out=sm, in0=sm, scalar1=4 * EPS, scalar2=None, op0=ALU.add), r=[smk], w=[smk])
                S.pool("tensor_tensor", C(out=sm, in0=sm, in1=CONSTS[0:P, 2:3], op=ALU.pow), r=[smk, "CONSTS"], w=[smk])
                yield
                if not smp:
                    while t > 0 and sgu_done[t - 1] < 4:
                        yield
                    S.dve("scalar_tensor_tensor", C(out=VTM[:, tb, :], in0=g2[:, :], scalar=sm, in1=VNB[:, :],
                                                    op0=ALU.mult, op1=ALU.mult), r=[g2k, smk, "VNB"], w=[("A", "VTM", tb)])
                    vtm_cnt[t] += 1
                    yield
                else:
                    vs, vk = TSP[p]
                    S.dve("scalar_tensor_tensor", C(out=vs[0:NS, :], in0=g2[0:NS, :], scalar=sm, in1=VNB[0:NS, :],
                                                    op0=ALU.mult, op1=ALU.mult), r=[g2k, smk, "VNB"], w=[vk])
                    yield
                    S.dma("dma_start", C(out=O["sgu_v"][e_], in_=vs[0:NS, :]), r=[vk])
                    while not bank_free:
                        yield
                    pbv, pkv = bank()
                    for j in range(4):
                        S.pe("transpose", C(out=pbv[:, j * 16:(j + 1) * 16], in_=vs[0:NS, j * 128:(j + 1) * 128],
                                            identity=IDENT[0:NS, 0:NS]), r=[vk, "IDENT"], w=[pkv])
                    S.dve("tensor_copy", C(out=VFM[:, :], in_=pbv[:, 0:64]), r=[pkv], w=["VFM"])
                    rel(pkv)
                    v_done[t] = True
                    yield

            def u_chain(t, j, p):
                c0, N = TILES[t]
                smp = (t == 4)
                while not norm_done[t]:
                    yield
                while not bank_free:
                    yield
                pb, pk = bank()
                for k in range(8):
                    S.pe("matmul", C(out=pb[:, 0:N], lhsT=Wu[:, k, j * 128:(j + 1) * 128], rhs=XNT[:, k, 0:N],
                                     start=(k == 0), stop=(k == 7)), r=[xnt_keys[k], Ku], w=[pk])
                xnt_reads[t] += 1
                yield
                u2, u2k = G2U[p]
                gel, gk = GEL_VU[p]
                yield from gelu2(pb[:, 0:N], pk, u2[:, 0:N], u2k, gel, gk, 128, N)
                rel(pk)
                ts_, tsk = TSP[p]
                if not smp:
                    while vtm_cnt[t] < 4:
                        yield
                    while not bank_free:
                        yield
                    pb2, pk2 = bank()
                    for tb in range(4):
                        for hh in range(2):
                            S.pe("matmul", C(out=pb2[hh * 64:(hh + 1) * 64, tb * 128:(tb + 1) * 128],
                                             lhsT=VTM[:, tb, (2 * j + hh) * 64:(2 * j + hh + 1) * 64], rhs=WMT[:, 2 * j + hh, :],
                                             start=True, stop=True), r=[("A", "VTM", tb), ("WMT", 2 * j + hh)], w=[pk2])
                    sgu_done[t] += 1
                    yield
                    S.dve("scalar_tensor_tensor", C(out=ts_[:, :].rearrange("p (a b) -> p a b", a=4),
                                                    in0=pb2[:, :].rearrange("p (a b) -> p a b", a=4), scalar=0.5,
                                                    in1=BH[:, j, :].unsqueeze(1).to_broadcast([128, 4, 128]),
                                                    op0=ALU.mult, op1=ALU.add), r=[pk2, "BH"], w=[tsk])
                    rel(pk2)
                    yield
                else:
                    while not v_done[t]:
                        yield
                    S.dve("tensor_scalar", C(out=ts_[:, 0:NS], in0=VFM[:, j * 16:(j + 1) * 16], scalar1=W0H[:, j:j + 1],
                                             scalar2=BH[:, j, 0:1], op0=ALU.mult, op1=ALU.add), r=["VFM", "W0H", "BH"], w=[tsk])
                    yield
                while t > 0 and w_mm[t - 1] < 8:
                    yield
                S.dve("tensor_tensor", C(out=CAT[:, j, 0:N], in0=ts_[:, 0:N], in1=u2[:, 0:N], op=ALU.mult),
                      r=[tsk, u2k], w=[("A", "CAT", j)])
                cat_cnt[t] += 1
                yield

            def g_chain(t, j, p):
                c0, N = TILES[t]
                while not norm_done[t]:
                    yield
                if j >= 2:
                    while not x_done[t][j - 2]:
                        yield
                elif t > 0:
                    while not x_done[t - 1][j + 2]:
                        yield
                while not bank_free:
                    yield
                pb, pk = bank()
                for k in range(8):
                    S.pe("matmul", C(out=pb[:, 0:N], lhsT=Wg[:, k, j * 128:(j + 1) * 128], rhs=XNT[:, k, 0:N],
                                     start=(k == 0), stop=(k == 7)), r=[xnt_keys[k], Kg], w=[pk])
                xnt_reads[t] += 1
                yield
                gel, gk = WR[:, 512:1024], ("WR", 1)
                gg2, ggk = GG2[p]
                yield from gelu2(pb[:, 0:N], pk, gg2[:, 0:N], ggk, gel, gk, 128, N)
                rel(pk)
                g_done[t][j] = True

            def l_chain(t, j, p):
                c0, N = TILES[t]
                smp = (t == 4)
                gg2, ggk = GG2[p]
                while not norm_done[t]:
                    yield
                while not bank_free:
                    yield
                pbx, pkx = bank()
                for k in range(8):
                    S.pe("matmul", C(out=pbx[:, 0:N], lhsT=Wx[:, k, j * 128:(j + 1) * 128], rhs=XNT[:, k, 0:N],
                                     start=(k == 0), stop=(k == 7)), r=[xnt_keys[k], Kx], w=[pkx])
                xnt_reads[t] += 1
                yield
                xc, xck = XCP[p]
                xbk = ("B", "XBUF", j)
                cws = [VECT[:, CW0 + e_ * 16 + k * 4 + j:CW0 + e_ * 16 + k * 4 + j + 1] for k in range(4)]
                if not smp:
                    S.act("activation", C(out=XBUF[:, j, 3:3 + N], in_=pbx[:, 0:N], func=AF.Copy), r=[pkx], w=[xbk])
                    S.act("activation", C(out=xc[:, 0:N], in_=pbx[:, 0:N], func=AF.Identity, bias=cvec(CB0, j), scale=cws[3]),
                          r=[pkx, "VECT"], w=[xck])
                    rel(pkx)
                    yield
                    for k in range(3):
                        S.dve("scalar_tensor_tensor", C(out=xc[:, 0:N], in0=XBUF[:, j, k:k + N], scalar=cws[k], in1=xc[:, 0:N],
                                                        op0=ALU.mult, op1=ALU.add), r=[xbk, xck, "VECT"], w=[xck])
                        yield
                    S.pool("tensor_copy", C(out=XBUF[:, j, 0:3], in_=XBUF[:, j, N:N + 3]), r=[xbk, xck], w=[xbk])
                else:
                    S.act("activation", C(out=XBS[:, j, :], in_=pbx[:, 0:N], func=AF.Copy), r=[pkx], w=[("XBS", j)])
                    rel(pkx)
                    yield
                    S.pool("tensor_scalar", C(out=xc[:, 0:N], in0=XBS[:, j, :], scalar1=cws[3], scalar2=cvec(CB0, j),
                                              op0=ALU.mult, op1=ALU.add), r=[("XBS", j), "VECT"], w=[xck])
                    yield
                    for k in range(3):
                        S.dve("scalar_tensor_tensor", C(out=xc[:, 0:N], in0=HIST[:, k * 4 + j, :], scalar=cws[k], in1=xc[:, 0:N],
                                                        op0=ALU.mult, op1=ALU.add), r=[("HIST", k), xck, "VECT"], w=[xck])
                        yield
                xcb, xcbk = XCBP[p]
                S.act("activation", C(out=xcb[:, 0:N], in_=xc[:, 0:N], func=AF.Copy), r=[xck], w=[xcbk])
                yield
                while len(bank_free) < 2:
                    yield
                pbr, pkr = bank()
                S.pe("matmul", C(out=pbr[:, 0:N], lhsT=WBD[:, 0, j, :], rhs=xcb[:, 0:N], start=True, stop=True),
                     r=[xcbk, "WBD"] + wbd_keys, w=[pkr])
                pbi, pki = bank()
                S.pe("matmul", C(out=pbi[:, 0:N], lhsT=WBD[:, 1, j, :], rhs=xcb[:, 0:N], start=True, stop=True),
                     r=[xcbk, "WBD"] + wbd_keys, w=[pki])
                yield
                thr, thrk = THR[p]
                thi, thik = THI[p]
                a_, ak = GEL_L[p]
                a2, a2k = A2P[p]
                S.act("activation", C(out=thr[:, 0:N], in_=pbr[:, 0:N], func=AF.Tanh, bias=dvec(DHBA0, j), scale=0.5),
                      r=[pkr, "DV"], w=[thrk])
                rel(pkr)
                S.act("activation", C(out=thi[:, 0:N], in_=pbi[:, 0:N], func=AF.Tanh, bias=dvec(DHBX0, j), scale=0.5),
                      r=[pki, "DV"], w=[thik])
                rel(pki)
                yield
                S.act("activation", C(out=a_[:, 0:N], in_=thr[:, 0:N], func=AF.Exp, bias=dvec(DHC0, j), scale=dvec(DHC0, j)),
                      r=[thrk, "DV"], w=[ak])
                S.act("activation", C(out=a2[:, 0:N], in_=thr[:, 0:N], func=AF.Exp, bias=dvec(DC0, j), scale=dvec(DC0, j)),
                      r=[thrk, "DV"], w=[a2k])
                S.dve("scalar_tensor_tensor", C(out=thi[:, 0:N], in0=thi[:, 0:N], scalar=1.0, in1=xc[:, 0:N],
                                                op0=ALU.add, op1=ALU.mult), r=[thik, xck], w=[thik])
                yield
                S.act("activation", C(out=a2[:, 0:N], in_=a2[:, 0:N], func=AF.Relu, bias=1.0, scale=-1.0), r=[a2k], w=[a2k])
                S.act("activation", C(out=a2[:, 0:N], in_=a2[:, 0:N], func=AF.Ln, bias=CONSTS[:, 3:4], scale=1.0),
                      r=[a2k, "CONSTS"], w=[a2k])
                yield
                S.act("activation", C(out=a2[:, 0:N], in_=a2[:, 0:N], func=AF.Exp, scale=0.5), r=[a2k], w=[a2k])
                yield
                S.dve("scalar_tensor_tensor", C(out=thi[:, 0:N], in0=thi[:, 0:N], scalar=0.25, in1=a2[:, 0:N],
                                                op0=ALU.mult, op1=ALU.mult), r=[thik, a2k], w=[thik])
                yield
                hs, hsk = THR[p]
                if not smp:
                    S.dve("tensor_tensor_scan", C(out=hs[:, 0:N], data0=a_[:, 0:N], data1=thi[:, 0:N],
                                                  initial=HCAR[:, j:j + 1], op0=ALU.mult, op1=ALU.add),
                          r=[ak, thik, ("HCAR", j)], w=[hsk])
                    yield
                    S.pool("tensor_copy", C(out=HCAR[:, j:j + 1], in_=hs[:, N - 1:N]), r=[hsk, ("HCAR", j)], w=[("HCAR", j)])
                else:
                    S.dve("tensor_tensor", C(out=hs[:, 0:N], in0=a_[:, 0:N], in1=H0[:, j, :], op=ALU.mult), r=[ak, "H0"], w=[hsk])
                    yield
                    S.dve("tensor_tensor", C(out=hs[:, 0:N], in0=hs[:, 0:N], in1=thi[:, 0:N], op=ALU.add), r=[hsk, thik], w=[hsk])
                    yield
                    S.pool("tensor_scalar", C(out=HSS[:, j, :], in0=hs[:, 0:N], scalar1=2.0, scalar2=None, op0=ALU.mult),
                           r=[hsk], w=[("HSS", j)])
                    yield
                while (t > 0 and w_mm[t - 1] < 8) or not g_done[t][j]:
                    yield
                S.dve("tensor_tensor", C(out=CAT[:, 4 + j, 0:N], in0=hs[:, 0:N], in1=gg2[:, 0:N], op=ALU.mult),
                      r=[hsk, ggk], w=[("A", "CAT", 4 + j)])
                cat_cnt[t] += 1
                x_done[t][j] = True
                yield

            def w_chain(t, m):
                c0, N = TILES[t]
                while cat_cnt[t] < 8:
                    yield
                while not bank_free:
                    yield
                pb, pk = bank()
                for k in range(8):
                    S.pe("matmul", C(out=pb[:, 0:N], lhsT=WO[m // 4][:, k, (m % 4) * 128:(m % 4 + 1) * 128],
                                     rhs=CAT[:, k, 0:N], start=(k == 0), stop=(k == 7)),
                         r=[("A", "CAT", k), ("B", "WO", m // 4)], w=[pk])
                w_mm[t] += 1
                yield
                S.dve("tensor_tensor", C(out=X[:, m, c0:c0 + N], in0=X[:, m, c0:c0 + N], in1=pb[:, 0:N], op=ALU.add),
                      r=[pk, ("X", m, t)], w=[("X", m, t)])
                rel(pk)
                yield

            def us_chain(p):
                t, N = 4, NS
                while not norm_done[t] or not v_done[t]:
                    yield
                while not bank_free:
                    yield
                pb, pk = bank()
                for j in range(4):
                    for k in range(8):
                        S.pe("matmul", C(out=pb[:, j * NS:(j + 1) * NS], lhsT=Wu[:, k, j * 128:(j + 1) * 128], rhs=XNT[:, k, 0:N],
                                         start=(k == 0), stop=(k == 7)), r=[xnt_keys[k], Ku], w=[pk])
                xnt_reads[t] += 4
                yield
                u2, u2k = G2U[p]
                gel, gk = GEL_VU[p]
                yield from gelu2(pb[:, 0:4 * NS], pk, u2[:, 0:4 * NS], u2k, gel, gk, 128, 4 * NS)
                rel(pk)
                ts_, tsk = TSP[p]
                for j in range(4):
                    S.dve("tensor_scalar", C(out=ts_[:, j * NS:(j + 1) * NS], in0=VFM[:, j * NS:(j + 1) * NS], scalar1=W0H[:, j:j + 1],
                                             scalar2=BH[:, j, 0:1], op0=ALU.mult, op1=ALU.add), r=["VFM", "W0H", "BH", tsk], w=[tsk])
                yield
                while w_mm[t - 1] < 8:
                    yield
                S.dve("tensor_tensor", C(out=CAT[:, 0:4, 0:N], in0=ts_[:, 0:4 * NS].rearrange("p (j n) -> p j n", j=4),
                                         in1=u2[:, 0:4 * NS].rearrange("p (j n) -> p j n", j=4), op=ALU.mult),
                      r=[tsk, u2k], w=[("A", "CAT", j) for j in range(4)])
                cat_cnt[t] += 4
                yield

            def gs_chain():
                t, N = 4, NS
                while not norm_done[t] or not x_done[t - 1][2]:
                    yield
                while not bank_free:
                    yield
                pb, pk = bank()
                for j in range(4):
                    for k in range(8):
                        S.pe("matmul", C(out=pb[:, j * NS:(j + 1) * NS], lhsT=Wg[:, k, j * 128:(j + 1) * 128], rhs=XNT[:, k, 0:N],
                                         start=(k == 0), stop=(k == 7)), r=[xnt_keys[k], Kg], w=[pk])
                xnt_reads[t] += 4
                yield
                gel, gk = WR[:, 512:1024], ("WR", 1)
                gg2, ggk = GG2[0]
                yield from gelu2(pb[:, 0:4 * NS], pk, gg2[:, 0:4 * NS], ggk, gel, gk, 128, 4 * NS)
                rel(pk)
                for j in range(4):
                    g_done[t][j] = True

            def ls_chain():
                t, N, W4 = 4, NS, 4 * NS
                p = 0
                gg2, ggk = GG2[0]
                while not norm_done[t]:
                    yield
                while not bank_free:
                    yield
                pbx, pkx = bank()
                for j in range(4):
                    for k in range(8):
                        S.pe("matmul", C(out=pbx[:, j * NS:(j + 1) * NS], lhsT=Wx[:, k, j * 128:(j + 1) * 128], rhs=XNT[:, k, 0:N],
                                         start=(k == 0), stop=(k == 7)), r=[xnt_keys[k], Kx], w=[pkx])
                xnt_reads[t] += 4
                yield
                xc, xck = XCP[p]
                xbs_keys = [("XBS", j) for j in range(4)]
                S.act("activation", C(out=XBS[:, :, :], in_=pbx[:, 0:W4].rearrange("p (j n) -> p j n", j=4), func=AF.Copy),
                      r=[pkx], w=xbs_keys)
                rel(pkx)
                yield
                cw = lambda k, j: VECT[:, CW0 + e_ * 16 + k * 4 + j:CW0 + e_ * 16 + k * 4 + j + 1]
                for j in range(4):
                    S.dve("tensor_scalar", C(out=xc[:, j * NS:(j + 1) * NS], in0=XBS[:, j, :], scalar1=cw(3, j), scalar2=cvec(CB0, j),
                                             op0=ALU.mult, op1=ALU.add), r=[("XBS", j), "VECT", xck], w=[xck])
                yield
                for k in range(3):
                    for j in range(4):
                        S.dve("scalar_tensor_tensor", C(out=xc[:, j * NS:(j + 1) * NS], in0=HIST[:, k * 4 + j, :], scalar=cw(k, j),
                                                        in1=xc[:, j * NS:(j + 1) * NS], op0=ALU.mult, op1=ALU.add),
                              r=[("HIST", k), xck, "VECT"], w=[xck])
                    yield
                xcb, xcbk = XCBP[p]
                S.act("activation", C(out=xcb[:, 0:W4], in_=xc[:, 0:W4], func=AF.Copy), r=[xck], w=[xcbk])
                yield
                while len(bank_free) < 2:
                    yield
                pbr, pkr = bank()
                pbi, pki = bank()
                for j in range(4):
                    S.pe("matmul", C(out=pbr[:, j * NS:(j + 1) * NS], lhsT=WBD[:, 0, j, :], rhs=xcb[:, j * NS:(j + 1) * NS],
                                     start=True, stop=True), r=[xcbk, "WBD"] + wbd_keys, w=[pkr])
                    S.pe("matmul", C(out=pbi[:, j * NS:(j + 1) * NS], lhsT=WBD[:, 1, j, :], rhs=xcb[:, j * NS:(j + 1) * NS],
                                     start=True, stop=True), r=[xcbk, "WBD"] + wbd_keys, w=[pki])
                yield
                thr, thrk = THR[p]
                thi, thik = THI[p]
                a_, ak = GEL_L[p]
                a2, a2k = A2P[p]
                for j in range(4):
                    sl = slice(j * NS, (j + 1) * NS)
                    S.act("activation", C(out=thr[:, sl], in_=pbr[:, sl], func=AF.Tanh, bias=dvec(DHBA0, j), scale=0.5),
                          r=[pkr, "DV", thrk], w=[thrk])
                    S.act("activation", C(out=thi[:, sl], in_=pbi[:, sl], func=AF.Tanh, bias=dvec(DHBX0, j), scale=0.5),
                          r=[pki, "DV", thik], w=[thik])
                rel(pkr)
                rel(pki)
                yield
                for j in range(4):
                    sl = slice(j * NS, (j + 1) * NS)
                    S.act("activation", C(out=a_[:, sl], in_=thr[:, sl], func=AF.Exp, bias=dvec(DHC0, j), scale=dvec(DHC0, j)),
                          r=[thrk, "DV", ak], w=[ak])
                    S.act("activation", C(out=a2[:, sl], in_=thr[:, sl], func=AF.Exp, bias=dvec(DC0, j), scale=dvec(DC0, j)),
                          r=[thrk, "DV", a2k], w=[a2k])
                S.dve("scalar_tensor_tensor", C(out=thi[:, 0:W4], in0=thi[:, 0:W4], scalar=1.0, in1=xc[:, 0:W4],
                                                op0=ALU.add, op1=ALU.mult), r=[thik, xck], w=[thik])
                yield
                S.act("activation", C(out=a2[:, 0:W4], in_=a2[:, 0:W4], func=AF.Relu, bias=1.0, scale=-1.0), r=[a2k], w=[a2k])
                S.act("activation", C(out=a2[:, 0:W4], in_=a2[:, 0:W4], func=AF.Ln, bias=CONSTS[:, 3:4], scale=1.0),
                      r=[a2k, "CONSTS"], w=[a2k])
                yield
                S.act("activation", C(out=a2[:, 0:W4], in_=a2[:, 0:W4], func=AF.Exp, scale=0.5), r=[a2k], w=[a2k])
                yield
                S.dve("scalar_tensor_tensor", C(out=thi[:, 0:W4], in0=thi[:, 0:W4], scalar=0.25, in1=a2[:, 0:W4],
                                                op0=ALU.mult, op1=ALU.mult), r=[thik, a2k], w=[thik])
                yield
                hs, hsk = THR[p]
                S.dve("tensor_tensor", C(out=hs[:, 0:W4], in0=a_[:, 0:W4], in1=H0[:, :, :].rearrange("p j n -> p (j n)"), op=ALU.mult),
                      r=[ak, "H0"], w=[hsk])
                yield
                S.dve("tensor_tensor", C(out=hs[:, 0:W4], in0=hs[:, 0:W4], in1=thi[:, 0:W4], op=ALU.add), r=[hsk, thik], w=[hsk])
                yield
                S.pool("tensor_scalar", C(out=HSS[:, :, :], in0=hs[:, 0:W4].rearrange("p (j n) -> p j n", j=4), scalar1=2.0, scalar2=None,
                                          op0=ALU.mult), r=[hsk], w=[("HSS", j) for j in range(4)])
                while w_mm[t - 1] < 8 or not g_done[t][0]:
                    yield
                S.dve("tensor_tensor", C(out=CAT[:, 4:8, 0:N], in0=hs[:, 0:W4].rearrange("p (j n) -> p j n", j=4),
                                         in1=gg2[:, 0:W4].rearrange("p (j n) -> p j n", j=4), op=ALU.mult),
                      r=[hsk, ggk], w=[("A", "CAT", 4 + j) for j in range(4)])
                cat_cnt[t] += 4
                for j in range(4):
                    x_done[t][j] = True
                yield

            nc_, vu_, gc_, lc_, wc_ = [], [], [], [], []
            vu_i = 0
            for t in range(NTL):
                if t > 0:
                    nc_.append(n_chain(t))
                if t < 4:
                    for tb in range(4):
                        vu_.append(v_chain(t, tb, vu_i % 2))
                        vu_i += 1
                else:
                    vu_.append(v_chain(t, 0, vu_i % 2))
                    vu_i += 1
                if t < 4:
                    for j in range(4):
                        vu_.append(u_chain(t, j, vu_i % 2))
                        vu_i += 1
                    for j in range(4):
                        gc_.append(g_chain(t, j, j % 2))
                        lc_.append(l_chain(t, j, j % 2))
                else:
                    vu_.append(us_chain(vu_i % 2))
                    vu_i += 1
                    gc_.append(gs_chain())
                    lc_.append(ls_chain())
                for m in range(8):
                    wc_.append(w_chain(t, m))
            run_multi([(wc_, 2), (nc_, 1), (vu_, 2), (gc_, 1), (lc_, 2)])
            for u in (uv, ug, ux, uu):
                ws_release(u)
            out_T([XBUF[:, j, 0:3] for j in range(4)], [("B", "XBUF", j) for j in range(4)], 3,
                  lambda g: O["conv_p"][e_])
            S.dve("tensor_scalar", C(out=SMALL[:, 4:8], in0=HCAR[:, 0:4], scalar1=2.0, scalar2=None, op0=ALU.mult),
                  r=[("HCAR", j) for j in range(4)], w=["HOUT"])
            out_T([SMALL[:, 4 + j:5 + j] for j in range(4)], ["HOUT"] * 4, 1,
                  lambda g: O["h_p"][e_:e_ + 1, :])
            out_T([XBS[:, j, :] for j in range(4)], [("XBS", j) for j in range(4)], NS,
                  lambda g: O["conv_s"][e_, :, 2, :])
            out_T([HSS[:, j, :] for j in range(4)], [("HSS", j) for j in range(4)], NS,
                  lambda g: O["h_s"][e_])

        def odd_mixer(l):
            o_ = l // 2
            fence_region("B")
            S.dma("dma_start", C(out=PWF, in_=I["pool_w"][o_].rearrange("g (ki p) n -> p g ki n", p=128)), w=[("B", "PWF")])
            S.dma("dma_start", C(out=PSB, in_=I["pool_scale"][o_].partition_broadcast(128)), w=[("B", "PSB")])
            def fold_scale():
                for g in range(4):
                    S.dve("scalar_tensor_tensor", C(out=PW[:, g, :, :], in0=PWF[:, g, :, :], scalar=(0.5 if g == 0 else 1.0),
                                                    in1=PSB[:, g * 256:(g + 1) * 256].unsqueeze(1).to_broadcast([128, 2, 256]),
                                                    op0=ALU.mult, op1=ALU.mult),
                          r=[("B", "PWF"), ("B", "PSB")], w=[("PW", g)])
            for h in range(2):
                S.dma("dma_start", C(out=SPT[h][0:120, :], in_=I["st_pool"][o_, h * 8:(h + 1) * 8].rearrange("b j d -> (b j) d")),
                      w=[("B", "SPT", h)])
            pvec = lambda base, c: VECT[:, base + o_ * 8 + c:base + o_ * 8 + c + 1]

            def do_norm(tn):
                odd_norm(l, tn)

            for t, (c0, N) in enumerate(TILES):
                smp = (t == 4)
                for c in range(8):
                    g = c // 2
                    w = 2 << g
                    if not smp and g == 0:
                        S.dve("tensor_tensor", C(out=PBUF[:, c, 0:N], in0=XNF[:, c, 14:14 + N], in1=XNF[:, c, 15:15 + N],
                                                 op=ALU.subtract), r=[("A", "XNF", c)], w=[("B", "PBUF", c)])
                        if t == 0:
                            S.pool("memset", C(PBUF[:, c, 0:1], 0.0), r=[("B", "PBUF", c)], w=[("B", "PBUF", c)])
                    elif not smp:
                        L = 15 + N
                        cur, curk = XNF[:, c, :], ("A", "XNF", c)
                        off = 0
                        on_pool = False
                        eng = S.pool if on_pool else S.dve
                        base = 2 if on_pool else 0
                        for step in range(g + 1):
                            sh = 1 << step
                            nxt, nk = SW[base + step % 2], ("A", "SW", base + step % 2)
                            eng("tensor_tensor", C(out=nxt[:, off + sh:L], in0=cur[:, off + sh:L], in1=cur[:, off:L - sh], op=ALU.add),
                                r=[curk], w=[nk])
                            cur, curk = nxt, nk
                            off += sh
                        if on_pool:
                            S.pool("tensor_scalar", C(out=cur[:, 15:15 + N], in0=cur[:, 15:15 + N], scalar1=1.0 / w, scalar2=None,
                                                      op0=ALU.mult), r=[curk], w=[curk])
                            if t == 0:
                                S.pool("tensor_tensor", C(out=cur[:, 15:15 + w - 1], in0=cur[:, 15:15 + w - 1], in1=WCNT[:, g, 0:w - 1],
                                                          op=ALU.mult), r=[curk, "WCNT"], w=[curk])
                            S.pool("tensor_tensor", C(out=PBUF[:, c, 0:N], in0=cur[:, 15:15 + N], in1=XNF[:, c, 15:15 + N],
                                                      op=ALU.subtract), r=[curk, ("A", "XNF", c)], w=[("B", "PBUF", c)])
                        else:
                            S.dve("scalar_tensor_tensor", C(out=PBUF[:, c, 0:N], in0=cur[:, 15:15 + N], scalar=1.0 / w,
                                                            in1=XNF[:, c, 15:15 + N], op0=ALU.mult, op1=ALU.subtract),
                                  r=[curk, ("A", "XNF", c)], w=[("B", "PBUF", c)])
                            if t == 0:
                                tf, tfk = T[4], ("T", 4)
                                S.dve("tensor_tensor", C(out=tf[:, 0:w - 1], in0=cur[:, 15:15 + w - 1], in1=INVCNT[:, 0:w - 1], op=ALU.mult),
                                      r=[curk, "INVCNT"], w=[tfk])
                                S.dve("tensor_tensor", C(out=PBUF[:, c, 0:w - 1], in0=tf[:, 0:w - 1], in1=XNF[:, c, 15:15 + w - 1],
                                                         op=ALU.subtract), r=[tfk, ("A", "XNF", c), ("B", "PBUF", c)], w=[("B", "PBUF", c)])
                    else:
                        pb, pk = bank()
                        for h in range(2):
                            S.pe("matmul", C(out=pb[:, h * 8:(h + 1) * 8], lhsT=SPT[h][0:120, c * 128:(c + 1) * 128],
                                             rhs=SEL[0:120, g, :], start=True, stop=True), r=[("B", "SPT", h), "SEL"], w=[pk])
                        tf, tfk = T[4], ("T", 4)
                        S.dve("tensor_tensor", C(out=tf[:, 0:NS], in0=pb[:, 0:NS], in1=XNS[:, c, :], op=ALU.add),
                              r=[pk, ("A", "XNS", c)], w=[tfk])
                        rel(pk)
                        if g == 0:
                            S.dve("scalar_tensor_tensor", C(out=PBUF[:, c, 0:NS], in0=XNS[:, c, :], scalar=-2.0, in1=tf[:, 0:NS],
                                                            op0=ALU.mult, op1=ALU.add), r=[tfk, ("A", "XNS", c)], w=[("B", "PBUF", c)])
                        else:
                            S.dve("scalar_tensor_tensor", C(out=PBUF[:, c, 0:NS], in0=tf[:, 0:NS], scalar=1.0 / w, in1=XNS[:, c, :],
                                                            op0=ALU.mult, op1=ALU.subtract), r=[tfk, ("A", "XNS", c)], w=[("B", "PBUF", c)])
                if not smp:
                    S.pool("tensor_copy", C(out=XNF[:, :, 0:15], in_=XNF[:, :, N:N + 15]),
                           r=[("A", "XNF", c) for c in range(8)], w=[("A", "XNF", c) for c in range(8)])
                if t == 3:
                    out_T([XNF[:, c, 0:15] for c in range(8)], [("A", "XNF", c) for c in range(8)], 15,
                          lambda g: O["pool_p"][o_, :, g * 512:(g + 1) * 512])
                if t + 1 < 5:
                    do_norm(t + 1)
                if t == 0:
                    fold_scale()
                for m in range(8):
                    g, mo = m // 2, m % 2
                    pb, pk = bank()
                    for ki in range(2):
                        S.pe("matmul", C(out=pb[:, 0:N], lhsT=PW[:, g, ki, mo * 128:(mo + 1) * 128], rhs=PBUF[:, 2 * g + ki, 0:N],
                                         start=(ki == 0), stop=False), r=[("B", "PBUF", 2 * g + ki), ("PW", g)], w=[pk])
                    S.pe("matmul", C(out=pb[:, 0:N], lhsT=IDENT[:], rhs=X[:, m, c0:c0 + N], start=False, stop=True),
                         r=[("X", m, t), "IDENT"], w=[pk])
                    S.act("activation", C(out=X[:, m, c0:c0 + N], in_=pb[:, 0:N], func=AF.Identity,
                                          bias=DV[:, DPBS0 + o_ * 8 + m:DPBS0 + o_ * 8 + m + 1], scale=1.0),
                          r=[pk, ("X", m, t), "DV"], w=[("X", m, t)])
                    rel(pk)
                if t == 4:
                    out_T([XNS[:, c, :] for c in range(8)], [("A", "XNS", c) for c in range(8)], NS,
                          lambda g: O["pool_s"][o_, :, 14, g * 512:(g + 1) * 512])

        def ffn(l, after_tile=None, mid_hook=None, mid_hook2=None):
            fence_all()
            for t in range(5):
                c0, N = TILES[t]
                norm_tile(t, FFN0 + l * 8, lambda c: (XN[:, c, c0:c0 + N], ("A", "XN", c, t)))
            rr = 0
            for q in range(4):
                for h in range(2):
                    u, slot = ws_acquire()
                    W = RING[slot]
                    if q == 0 and h == 0:
                        order = [(mi, t) for t in range(5) for mi in range(4)]
                    else:
                        order = [(mi, t) for mi in range(4) for t in range(5)]
                    for mi, t in order:
                        m = h * 4 + mi
                        c0, N = TILES[t]
                        pb, pk = bank()
                        for k in range(8):
                            S.pe("matmul", C(out=pb[:, 0:N], lhsT=W[:, k, mi * 128:(mi + 1) * 128], rhs=XN[:, k, c0:c0 + N],
                                             start=(k == 0), stop=(k == 7)), r=[("A", "XN", k, t), ("R", slot)], w=[pk])
                        ti = rr % 4
                        rr += 1
                        tt, tk = T[ti], ("T", ti)
                        S.act("activation", C(out=tt[:, 0:N], in_=pb[:, 0:N], func=AF.Relu), r=[pk], w=[tk])
                        rel(pk)
                        eng = S.pool if rr % 3 else S.dve
                        eng("tensor_tensor", C(out=HT[:, m, c0:c0 + N], in0=tt[:, 0:N], in1=tt[:, 0:N], op=ALU.mult),
                            r=[tk], w=[("B", "HT", m, t)])
                    ws_release(u)
                if q == 1 and mid_hook is not None:
                    mid_hook()
                if q == 2 and mid_hook2 is not None:
                    mid_hook2()
                for h in range(2):
                    u, slot = ws_acquire()
                    W = RING[slot]
                    if q == 3 and h == 1:
                        order = [(mi, t) for t in range(5) for mi in range(4)]
                    else:
                        order = [(mi, t) for mi in range(4) for t in range(5)]
                    for mi, t in order:
                        m = h * 4 + mi
                        c0, N = TILES[t]
                        pb, pk = bank()
                        for k in range(8):
                            S.pe("matmul", C(out=pb[:, 0:N], lhsT=W[:, k, mi * 128:(mi + 1) * 128], rhs=HT[:, k, c0:c0 + N],
                                             start=(k == 0), stop=(k == 7)), r=[("B", "HT", k, t), ("R", slot)], w=[pk])
                        S.dve("tensor_tensor", C(out=X[:, m, c0:c0 + N], in0=X[:, m, c0:c0 + N], in1=pb[:, 0:N], op=ALU.add),
                              r=[pk, ("X", m, t)], w=[("X", m, t)])
                        rel(pk)
                        if q == 3 and h == 1 and mi == 3 and after_tile is not None:
                            after_tile(t)
                    ws_release(u)

        def final_tile(t):
            c0, N = TILES[t]
            if t == 0:
                fence_region("A")
                STG_CUR[0] = STG
                STG_KEYS[0] = [("A", "STG", i) for i in range(4)]
            norm_tile(t, FIN0, lambda c: (YF[:, c, 0:N], ("A", "YF", c)))
            if t < 4:
                for tb in range(4):
                    out_T([YF[:, c, tb * 128:(tb + 1) * 128] for c in range(8)], [("A", "YF", c) for c in range(8)], 128,
                          lambda g, tb=tb, t=t: O["y_p"][t * 512 + tb * 128:t * 512 + (tb + 1) * 128, g * 512:(g + 1) * 512])
            else:
                out_T([YF[:, c, 0:NS] for c in range(8)], [("A", "YF", c) for c in range(8)], NS,
                      lambda g: O["y_s"][:, g * 512:(g + 1) * 512])

        load_x_tile(0)
        mixer_begin(0)
        load_x_sample()
        even_setup_consts(0)
        for t in range(1, 4):
            load_x_tile(t)
        even_setup_state(0)
        for l in range(4):
            if l % 2 == 0:
                even_mixer(l)
            else:
                odd_mixer(l)
            if l < 3:
                ffn(l, after_tile=lambda t, l=l: mixer_begin(l + 1) if t == 0 else None,
                    mid_hook=(lambda: (even_setup_consts(2, part="load"), even_setup_state_load(2))) if l == 1 else None,
                    mid_hook2=(lambda: (even_setup_wmt_pe(), even_setup_state_pe(2))) if l == 1 else None)
            else:
                ffn(l, after_tile=final_tile)
        assert len(bank_free) == 8, f"leaked PSUM banks: {bank_free}"
        counts = S.emit()
    return nc, counts


_CACHE = {}


def kernel(**inputs):
    f32 = lambda a: np.ascontiguousarray(np.asarray(a, dtype=np.float32))
    inp = {k: f32(v) for k, v in inputs.items()}
    if "nc" not in _CACHE:
        _CACHE["nc"] = build()[0]
    nc = _CACHE["nc"]
    shared = {k: inp[k] for k in ("norm_mix", "norm_ffn", "norm_final", "w_in", "w_out", "v_norm", "sgu_w", "sgu_b",
                                  "conv_w", "conv_b", "gate_a_w", "gate_a_b", "gate_x_w", "gate_x_b", "lru_lambda",
                                  "pool_w", "pool_b", "pool_scale", "ffn_w1", "ffn_w2")}
    in_maps = []
    for i in range(NCORES):
        sl = slice(i * NS, (i + 1) * NS)
        m = dict(shared)
        m["x_p"] = f32(inp["x_prompt"][i])
        m["x_s"] = f32(inp["x_sample"][sl, 0, :])
        m["st_conv"] = f32(inp["state_conv"][:, sl])
        m["st_h"] = f32(inp["state_rglru"][:, sl])
        m["st_pool"] = f32(inp["state_pool"][:, sl])
        in_maps.append(m)
    res = run_bass_kernel_spmd(nc, in_maps, core_ids=list(range(NCORES)))
    R = res.results
    y_prompt = np.stack([R[i]["y_p"] for i in range(NCORES)], 0)
    y_sample = np.concatenate([R[i]["y_s"] for i in range(NCORES)], 0)[:, None, :]
    sgu_v = np.concatenate([R[i]["sgu_v"] for i in range(NCORES)], 1)[:, :, None, :]
    conv_p = np.stack([R[i]["conv_p"] for i in range(NCORES)], 1)
    conv_s = np.concatenate([R[i]["conv_s"] for i in range(NCORES)], 1)
    h_p = np.stack([R[i]["h_p"] for i in range(NCORES)], 1)
    h_s = np.concatenate([R[i]["h_s"] for i in range(NCORES)], 1)
    pool_p = np.stack([R[i]["pool_p"] for i in range(NCORES)], 1)
    pool_s = np.concatenate([R[i]["pool_s"] for i in range(NCORES)], 1)
    return (y_prompt.astype(np.float32), y_sample.astype(np.float32), sgu_v.astype(np.float32), conv_p.astype(np.float32),
            conv_s.astype(np.float32), h_p.astype(np.float32), h_s.astype(np.float32), pool_p.astype(np.float32),
            pool_s.astype(np.float32))
```
